# Optimizing a Trainium2 kernel written in Bass

```python
import math
import jax, jax.numpy as jnp
from jax import lax
import numpy as np

D_MODEL = 1024
BATCH = 16
SEQ = 4096
DEPTH = 1

DIFF_HEADS = 4
DIFF_HEAD_DIM = 64
DIFF_V_DIM = 2 * DIFF_HEAD_DIM
DIFF_ROT = DIFF_HEAD_DIM // 4
MLA_HEADS = 8
MLA_NOPE = 64
MLA_ROPE = 32
MLA_V = 64
MLA_Q_RANK = 384
MLA_KV_RANK = 256
D_FF = 4 * D_MODEL
N_BRANCHES = 2
ROPE_THETA = 500000.0
Q_BLOCK = 128
LN_EPS = 1e-5
RMS_EPS = 1e-6
MASK_VALUE = -1e30
ALPHA = (2.0 * DEPTH) ** 0.25
BETA = (8.0 * DEPTH) ** -0.25

DIFF_Q_COLS = DIFF_HEADS * 2 * DIFF_HEAD_DIM
DIFF_K_COLS = DIFF_HEADS * 2 * DIFF_HEAD_DIM
DIFF_V_COLS = DIFF_HEADS * DIFF_V_DIM
GATE_COLS = N_BRANCHES * D_MODEL
SPLIT_SIZES = (DIFF_Q_COLS, DIFF_K_COLS, DIFF_V_COLS, MLA_Q_RANK, MLA_KV_RANK, MLA_ROPE, GATE_COLS)
SPLIT_POINTS = tuple(int(s) for s in np.cumsum(SPLIT_SIZES)[:-1])
D_IN = int(sum(SPLIT_SIZES))
DIFF_OUT = DIFF_HEADS * DIFF_V_DIM
MLA_OUT = MLA_HEADS * MLA_V

kernel_name = "hybrid_diffattn_mla_gated_deepnorm"


def layer_norm(x, g, b):
    xf = x.astype(jnp.float32)
    mu = jnp.mean(xf, axis=-1, keepdims=True)
    var = jnp.mean(jnp.square(xf - mu), axis=-1, keepdims=True)
    return ((xf - mu) * lax.rsqrt(var + LN_EPS) * g.astype(jnp.float32) + b.astype(jnp.float32)).astype(x.dtype)


def rms_norm(x, g):
    xf = x.astype(jnp.float32)
    ms = jnp.mean(jnp.square(xf), axis=-1, keepdims=True)
    return (xf * lax.rsqrt(ms + RMS_EPS) * g.astype(jnp.float32)).astype(x.dtype)


def apply_rope(x, positions, rot_dim):
    half = rot_dim // 2
    inv_freq = jnp.power(ROPE_THETA, -jnp.arange(half, dtype=jnp.float32) / half)
    ang = positions.astype(jnp.float32)[:, :, None] * inv_freq
    ang = ang.reshape(ang.shape[:2] + (1,) * (x.ndim - 3) + (half,))
    cos = jnp.cos(ang).astype(x.dtype)
    sin = jnp.sin(ang).astype(x.dtype)
    x1 = x[..., :half]
    x2 = x[..., half:rot_dim]
    return jnp.concatenate([x1 * cos - x2 * sin, x2 * cos + x1 * sin, x[..., rot_dim:]], axis=-1)


def multi_map_causal_attention(q, k, v, map_w, scale):
    B, S, M, H, D = q.shape
    Dv = v.shape[-1]
    nb = S // Q_BLOCK
    qb = jnp.moveaxis(q.reshape(B, nb, Q_BLOCK, M, H, D), 1, 0)
    k_pos = jnp.arange(S)

    def one_block(args):
        i, q_i = args
        s = jnp.einsum('bqmhd,bkmhd->bmhqk', q_i, k).astype(jnp.float32) * scale
        q_pos = i * Q_BLOCK + jnp.arange(Q_BLOCK)
        mask = k_pos[None, :] <= q_pos[:, None]
        p = jax.nn.softmax(jnp.where(mask, s, MASK_VALUE), axis=-1)
        p = jnp.einsum('mh,bmhqk->bhqk', map_w.astype(jnp.float32), p)
        return jnp.einsum('bhqk,bkhd->bqhd', p.astype(v.dtype), v)

    out = lax.map(one_block, (jnp.arange(nb), qb))
    return jnp.moveaxis(out, 0, 1).reshape(B, S, H, Dv)


def token_mixer(x, positions, w_in, gate_b, diff_lambda, diff_subln_g, mla_q_norm_g, w_uq,
                mla_kv_norm_g, w_ukv, w_o_diff, w_o_mla, w_out, lambda_init):
    B, S, _ = x.shape
    z = jnp.einsum('bsd,de->bse', x, w_in)
    dq, dk, dv, cq, ckv, kr, gates = jnp.split(z, SPLIT_POINTS, axis=-1)

    dq = apply_rope(dq.reshape(B, S, DIFF_HEADS, 2, DIFF_HEAD_DIM), positions, DIFF_ROT).swapaxes(2, 3)
    dk = apply_rope(dk.reshape(B, S, DIFF_HEADS, 2, DIFF_HEAD_DIM), positions, DIFF_ROT).swapaxes(2, 3)
    dv = dv.reshape(B, S, DIFF_HEADS, DIFF_V_DIM)
    lam = diff_lambda.astype(jnp.float32)
    lam_full = jnp.exp(jnp.sum(lam[0] * lam[1])) - jnp.exp(jnp.sum(lam[2] * lam[3])) + lambda_init
    ones_h = jnp.ones((DIFF_HEADS,), jnp.float32)
    map_w = jnp.stack([ones_h, -lam_full * ones_h])
    o_diff = multi_map_causal_attention(dq, dk, dv, map_w, DIFF_HEAD_DIM ** -0.5)
    o_diff = rms_norm(o_diff, diff_subln_g) * (1.0 - lambda_init)
    o_diff = o_diff.reshape(B, S, DIFF_OUT)

    q = jnp.einsum('bsr,re->bse', rms_norm(cq, mla_q_norm_g), w_uq).reshape(B, S, MLA_HEADS, MLA_NOPE + MLA_ROPE)
    q = jnp.concatenate([q[..., :MLA_NOPE], apply_rope(q[..., MLA_NOPE:], positions, MLA_ROPE)], axis=-1)
    kv = jnp.einsum('bsr,re->bse', rms_norm(ckv, mla_kv_norm_g), w_ukv).reshape(B, S, MLA_HEADS, MLA_NOPE + MLA_V)
    k_nope, v = kv[..., :MLA_NOPE], kv[..., MLA_NOPE:]
    k_rope = apply_rope(kr[:, :, None, :], positions, MLA_ROPE)
    k = jnp.concatenate([k_nope, jnp.broadcast_to(k_rope, (B, S, MLA_HEADS, MLA_ROPE))], axis=-1)
    o_mla = multi_map_causal_attention(q[:, :, None], k[:, :, None], v,
                                       jnp.ones((1, MLA_HEADS), jnp.float32),
                                       (MLA_NOPE + MLA_ROPE) ** -0.5)
    o_mla = o_mla.reshape(B, S, MLA_OUT)

    g = jax.nn.sigmoid(gates.reshape(B, S, N_BRANCHES, D_MODEL) + gate_b)
    y = g[:, :, 0] * jnp.einsum('bse,ed->bsd', o_diff, w_o_diff) \
        + g[:, :, 1] * jnp.einsum('bse,ed->bsd', o_mla, w_o_mla)
    return jnp.einsum('bsd,de->bse', y, w_out)


def squared_relu_mlp(x, w_up, w_down):
    h = jnp.square(jax.nn.relu(jnp.einsum('bsd,df->bsf', x, w_up)))
    return jnp.einsum('bsf,fd->bsd', h, w_down)


def setup_inputs(seed: int = 0) -> dict:
    key = jax.random.key(seed)
    ks = jax.random.split(key, 20)
    f32 = jnp.float32

    def nrm(k, shape, scale):
        return jax.random.normal(k, shape, f32) * scale

    def gain(k, shape):
        return 1.0 + 0.02 * jax.random.normal(k, shape, f32)

    x = jax.random.normal(ks[0], (BATCH, SEQ, D_MODEL), f32)
    offsets = jax.random.randint(ks[1], (BATCH, 1), 0, 1024, dtype=jnp.int32)
    positions = offsets + jnp.arange(SEQ, dtype=jnp.int32)[None, :]
    return {
        "x": x,
        "positions": positions,
        "w_in": nrm(ks[2], (DEPTH, D_MODEL, D_IN), D_MODEL ** -0.5),
        "gate_b": nrm(ks[3], (DEPTH, N_BRANCHES, D_MODEL), 0.02),
        "diff_lambda": nrm(ks[4], (DEPTH, 4, DIFF_HEAD_DIM), 0.1),
        "diff_subln_g": gain(ks[5], (DEPTH, DIFF_V_DIM)),
        "mla_q_norm_g": gain(ks[6], (DEPTH, MLA_Q_RANK)),
        "w_uq": nrm(ks[7], (DEPTH, MLA_Q_RANK, MLA_HEADS * (MLA_NOPE + MLA_ROPE)), MLA_Q_RANK ** -0.5),
        "mla_kv_norm_g": gain(ks[8], (DEPTH, MLA_KV_RANK)),
        "w_ukv": nrm(ks[9], (DEPTH, MLA_KV_RANK, MLA_HEADS * (MLA_NOPE + MLA_V)), MLA_KV_RANK ** -0.5),
        "w_o_diff": nrm(ks[10], (DEPTH, DIFF_OUT, D_MODEL), BETA * DIFF_OUT ** -0.5),
        "w_o_mla": nrm(ks[11], (DEPTH, MLA_OUT, D_MODEL), BETA * MLA_OUT ** -0.5),
        "w_out": nrm(ks[12], (DEPTH, D_MODEL, D_MODEL), BETA * D_MODEL ** -0.5),
        "ln1_g": gain(ks[13], (DEPTH, D_MODEL)),
        "ln1_b": nrm(ks[14], (DEPTH, D_MODEL), 0.02),
        "w_up": nrm(ks[15], (DEPTH, D_MODEL, D_FF), D_MODEL ** -0.5),
        "w_down": nrm(ks[16], (DEPTH, D_FF, D_MODEL), BETA * D_FF ** -0.5),
        "ln2_g": gain(ks[17], (DEPTH, D_MODEL)),
        "ln2_b": nrm(ks[18], (DEPTH, D_MODEL), 0.02),
    }


def reference(x, positions, w_in, gate_b, diff_lambda, diff_subln_g, mla_q_norm_g, w_uq,
              mla_kv_norm_g, w_ukv, w_o_diff, w_o_mla, w_out, ln1_g, ln1_b, w_up, w_down,
              ln2_g, ln2_b):
    for l in range(DEPTH):
        lambda_init = 0.8 - 0.6 * math.exp(-0.3 * l)
        h = token_mixer(x, positions, w_in[l], gate_b[l], diff_lambda[l], diff_subln_g[l],
                        mla_q_norm_g[l], w_uq[l], mla_kv_norm_g[l], w_ukv[l],
                        w_o_diff[l], w_o_mla[l], w_out[l], lambda_init)
        x = layer_norm(ALPHA * x + h, ln1_g[l], ln1_b[l])
        x = layer_norm(ALPHA * x + squared_relu_mlp(x, w_up[l], w_down[l]), ln2_g[l], ln2_b[l])
    return x
```

```python
import math
import numpy as np
import ml_dtypes
import concourse.bass as bass
import concourse.mybir as mybir
from concourse.bass_utils import run_bass_kernel_spmd

F32 = mybir.dt.float32
BF16 = mybir.dt.bfloat16
I32 = mybir.dt.int32
AF = mybir.ActivationFunctionType
ALU = mybir.AluOpType

S = 4096
D = 1024
NSEQ = 2
NG = 8
ALPHA = 2.0 ** 0.25
LAMBDA_INIT = 0.8 - 0.6 * math.exp(0.0)
NWA = 1712
NDMA = 32
TWO_PI = 2.0 * math.pi
C1 = 6.28125
C2 = TWO_PI - C1


class Prog:
    ENGS = ("pe", "act", "dve", "pool", "sp")

    def __init__(self):
        self.ops = {e: [] for e in self.ENGS}
        self.count = {e: 0 for e in self.ENGS}
        self.waited = {e: {} for e in self.ENGS}
        self.res_w = {}
        self.res_r = {}
        self.dma_val = [0] * NDMA
        self.dma_rr = 0
        self.marks = []

    def _deps(self, eng, r, w):
        need = {}

        def add(tok):
            if tok is None:
                return
            sk, v = tok
            if sk == "pe" and eng == "pe":
                return
            if v > need.get(sk, 0):
                need[sk] = v

        for k in r:
            add(self.res_w.get(k))
        for k in w:
            add(self.res_w.get(k))
            for sk, v in self.res_r.get(k, {}).items():
                add((sk, v))
        waits = []
        wd = self.waited[eng]
        for sk, v in need.items():
            if v > wd.get(sk, 0):
                wd[sk] = v
                waits.append((sk, v))
        return waits

    def _commit(self, tok, r, w):
        sk, v = tok
        for k in r:
            d = self.res_r.setdefault(k, {})
            if v > d.get(sk, 0):
                d[sk] = v
        for k in w:
            self.res_w[k] = tok
            self.res_r[k] = {}

    def op(self, eng, fn, r=(), w=()):
        waits = self._deps(eng, r, w)
        self.count[eng] += 1
        tok = (eng, self.count[eng])
        self._commit(tok, r, w)
        self.ops[eng].append((waits, fn, (eng, 1)))

    def dma(self, q, fn, r=(), w=()):
        waits = self._deps(q, r, w)
        i = self.dma_rr
        self.dma_rr = (i + 1) % NDMA
        sk = ("dma", i)
        prev = self.dma_val[i]
        wd = self.waited[q]
        if prev > wd.get(sk, 0):
            wd[sk] = prev
            waits.append((sk, prev))
        self.dma_val[i] = prev + 16
        tok = (sk, prev + 16)
        self._commit(tok, r, w)
        self.ops[q].append((waits, fn, (sk, 16)))

    def barrier(self):
        self.marks.append(dict(self.count))
        for e in self.ENGS:
            waits = []
            wd = self.waited[e]
            for o in ("pe", "act", "dve", "pool"):
                v = self.count[o]
                if o != e and v > wd.get(o, 0):
                    wd[o] = v
                    waits.append((o, v))
                if o == e and o != "pe" and v > wd.get(o, 0):
                    wd[o] = v
                    waits.append((o, v))
            for i in range(NDMA):
                sk = ("dma", i)
                v = self.dma_val[i]
                if v > wd.get(sk, 0):
                    wd[sk] = v
                    waits.append((sk, v))
            self.ops[e].append((waits, None, None))
        self.res_w = {}
        self.res_r = {}

    def emit(self, eng, e, sems):
        for waits, fn, inc in self.ops[eng]:
            for sk, v in waits:
                e.wait_ge(sems[sk], v)
            if fn is not None:
                ins = fn(e)
                ins.then_inc(sems[inc[0]], inc[1])


class Arena:
    def __init__(self, ap, n):
        self.ap = ap
        self.n = n
        self.off = 0

    def reset(self):
        self.off = 0

    def take(self, n):
        o = self.off
        self.off += n
        assert self.off <= self.n, (self.off, self.n)
        return self.ap[:, o:o + n]


def build():
    nc = bass.Bass("TRN2", target_bir_lowering=False)
    P = Prog()

    def din(name, shape, dt=F32):
        return nc.dram_tensor(name, shape, dt, kind="ExternalInput").ap()

    def dscr(name, shape, dt):
        return nc.dram_tensor(name, shape, dt, kind="Internal").ap()

    xT = din("xT", [NSEQ, D, S])
    x = din("x", [NSEQ, S, D])
    pos = din("pos", [NSEQ, S], I32)
    wa = din("wa", [128, 8 * NWA])
    wv = din("wv", [128, 8 * 512])
    wg = din("wg", [128, 8 * 2048])
    wuq = din("wuq", [128, 3 * 768])
    wukv = din("wukv", [128, 2 * 1024])
    wod = din("wod", [128, 4 * 1024])
    wom = din("wom", [128, 4 * 1024])
    wout = din("wout", [128, 8 * 1024])
    wup = din("wup", [128, 8 * 4096])
    wdn = din("wdn", [128, 32 * 1024])
    small = din("small", [128, 48])
    lam = din("lam", [1, 256])
    lnp = din("lnp", [4, 1024])
    cbf = din("cbf", [128, 256], BF16)
    y = nc.dram_tensor("y", [NSEQ * S, D], F32, kind="ExternalOutput").ap()

    fmd = dscr("fmd", [NSEQ, 1024, S], BF16)
    fmm = dscr("fmm", [NSEQ, 1312, S], BF16)
    vscr = dscr("vscr", [NSEQ, 2, 128, 32 * 512], BF16)
    xbs = dscr("xbs", [NSEQ, D, S], BF16)
    oTs = dscr("oTs", [NSEQ, D, S], BF16)
    x1s = dscr("x1s", [NSEQ * S, D], F32)
    x1Ts = dscr("x1Ts", [D, NSEQ * S], BF16)

    NBF = 69632
    NF = 15872
    abf_t = nc.alloc_sbuf_tensor("abf", [128, NBF], BF16)
    af_t = nc.alloc_sbuf_tensor("af", [128, NF], F32)
    smallt = nc.alloc_sbuf_tensor("smallt", [128, 48], F32)
    lamt = nc.alloc_sbuf_tensor("lamt", [128, 256], F32)
    lamw = nc.alloc_sbuf_tensor("lamw", [128, 8], F32)
    cbft = nc.alloc_sbuf_tensor("cbft", [128, 256], BF16)
    onest = nc.alloc_sbuf_tensor("onest", [128, 128], BF16)
    stt = nc.alloc_sbuf_tensor("stt", [128, 2, 2, 6], F32)
    mvt = nc.alloc_sbuf_tensor("mvt", [128, 2, 4], F32)
    psall_t = nc.alloc_psum_tensor("psall", [128, 4096], F32)
    psall = psall_t[:]
    ps = [psall[:, i * 512:(i + 1) * 512] for i in range(8)]

    def pspair(b0):
        return psall[:, b0 * 512:(b0 + 2) * 512].rearrange("p (a b) -> p a b", a=2, b=512)
    ABF = Arena(abf_t[:], NBF)
    AFF = Arena(af_t[:], NF)
    sm = smallt[:]
    maskb = cbft[:, 0:128]
    ident = cbft[:, 128:256]
    ones = onest[:]

    GQ, GKV, SUBG, INVFD, INVFM, GATEB, LN1G, LN1B = 0, 3, 5, 6, 7, 8, 24, 32
    NEGLAM, SUBGS = 0, 1

    P.dma("sp", lambda e: e.dma_start(out=sm, in_=small[:, :]), w=["small"])
    P.dma("sp", lambda e: e.dma_start(out=lamt[:], in_=lam[0:1, :].broadcast_to([128, 256])), w=["lamt"])
    P.dma("sp", lambda e: e.dma_start(out=cbft[:], in_=cbf[:, :]), w=["cbf"])
    P.op("pool", lambda e: e.memset(ones, 1.0), w=["ones"])
    lw = lamw[:]
    P.op("dve", lambda e: e.tensor_tensor(out=lamt[:, 0:64], in0=lamt[:, 0:64], in1=lamt[:, 64:128], op=ALU.mult), r=["lamt"], w=["lamA"])
    P.op("dve", lambda e: e.tensor_tensor(out=lamt[:, 128:192], in0=lamt[:, 128:192], in1=lamt[:, 192:256], op=ALU.mult), r=["lamt"], w=["lamB"])
    P.op("dve", lambda e: e.reduce_sum(out=lw[:, 2:3], in_=lamt[:, 0:64], axis=mybir.AxisListType.X), r=["lamA"], w=["lw2"])
    P.op("dve", lambda e: e.reduce_sum(out=lw[:, 3:4], in_=lamt[:, 128:192], axis=mybir.AxisListType.X), r=["lamB"], w=["lw3"])
    P.op("act", lambda e: e.activation(out=lw[:, 4:6], in_=lw[:, 2:4], func=AF.Exp), r=["lw2", "lw3"], w=["lw45"])
    P.op("dve", lambda e: e.tensor_tensor(out=lw[:, 6:7], in0=lw[:, 5:6], in1=lw[:, 4:5], op=ALU.subtract), r=["lw45"], w=["lw6"])
    P.op("dve", lambda e: e.tensor_scalar(out=lw[:, NEGLAM:NEGLAM + 1], in0=lw[:, 6:7], scalar1=-LAMBDA_INIT, scalar2=None, op0=ALU.add), r=["lw6"], w=["neglam"])
    P.op("dve", lambda e: e.tensor_scalar(out=lw[:, SUBGS:SUBGS + 1], in0=sm[:, SUBG:SUBG + 1], scalar1=1.0 - LAMBDA_INIT, scalar2=None, op0=ALU.mult), r=["small"], w=["subgs"])

    bank_rr = [0]
    WS = [None]

    def load_cast(dst3, src, C, N, scale_col=None, tag="w"):
        ws = WS[0]
        i = 0
        for c in range(C):
            for o in range(0, N, 1024):
                n = min(1024, N - o)
                slot = i % 2
                i += 1
                wsk = ("ws", slot)
                stage = ws[slot][:, 0:n]
                srcap = src[:, c * N + o:c * N + o + n]
                P.dma("sp", lambda e, a=stage, b=srcap: e.dma_start(out=a, in_=b), w=[wsk])
                dst = dst3[:, c, o:o + n]
                if scale_col is not None:
                    sc = sm[:, scale_col + c:scale_col + c + 1]
                    P.op("dve", lambda e, a=dst, b=stage, s_=sc: e.tensor_scalar(out=a, in0=b, scalar1=s_, scalar2=None, op0=ALU.mult),
                         r=[wsk, "small"], w=[(tag, c, o)])
                elif i % 2 == 0:
                    P.op("dve", lambda e, a=dst, b=stage: e.tensor_copy(out=a, in_=b), r=[wsk], w=[(tag, c, o)])
                else:
                    P.op("act", lambda e, a=dst, b=stage: e.activation(out=a, in_=b, func=AF.Copy), r=[wsk], w=[(tag, c, o)])

    def wkeys(tag, C, N):
        return [(tag, c, o) for c in range(C) for o in range(0, N, 1024)]

    def v3(ap, a, b):
        return ap.rearrange("p (a b) -> p a b", a=a, b=b)

    ABF.reset(); AFF.reset()
    WS[0] = [AFF.take(1024) for _ in range(2)]
    WA = v3(ABF.take(8 * NWA), 8, NWA)
    WV = v3(ABF.take(8 * 512), 8, 512)
    WUQ = v3(ABF.take(3 * 768), 3, 768)
    WUKV = v3(ABF.take(2 * 1024), 2, 1024)
    load_cast(WA, wa, 8, NWA, tag="WA")
    load_cast(WV, wv, 8, 512, tag="WV")
    load_cast(WUQ, wuq, 3, 768, scale_col=GQ, tag="WUQ")
    load_cast(WUKV, wukv, 2, 1024, scale_col=GKV, tag="WUKV")
    kWA = wkeys("WA", 8, NWA); kWV = wkeys("WV", 8, 512); kWUQ = wkeys("WUQ", 3, 768); kWUKV = wkeys("WUKV", 2, 1024)

    xbA = [v3(ABF.take(8 * 512), 8, 512) for _ in range(2)]
    stg = [ABF.take(512) for _ in range(8)]
    sqb = [ABF.take(512) for _ in range(2)]
    cqn = v3(ABF.take(3 * 512), 3, 512)
    ckvn = v3(ABF.take(2 * 512), 2, 512)
    vmt = v3(ABF.take(4 * 512), 4, 512)
    vdt = v3(ABF.take(4 * 512), 4, 512)
    xs = [AFF.take(512) for _ in range(4)]
    posi = AFF.take(512).bitcast(I32)
    posf = AFF.take(512)
    angA = AFF.take(512)
    ang2 = AFF.take(512)
    ki = AFF.take(512).bitcast(I32)
    kf = AFF.take(512)
    rr = AFF.take(512)
    COSD, SIND, COSM, SINM = (AFF.take(512) for _ in range(4))
    cqf = v3(AFF.take(3 * 512), 3, 512)
    ckvf = v3(AFF.take(2 * 512), 2, 512)
    lnt = AFF.take(512)
    rstd = AFF.take(512)
    tmp = [AFF.take(512) for _ in range(4)]
    stg_rr = [0]
    sq_rr = [0]

    def next_bank():
        b = bank_rr[0]
        bank_rr[0] = (b + 1) % 8
        return b

    def next_stg():
        i = stg_rr[0]
        stg_rr[0] = (i + 1) % 8
        return i

    def mm_group(bank, M, lhs_list, rhs_list, rkeys, ncols=512):
        n = len(lhs_list)
        for k in range(n):
            P.op("pe", lambda e, b=bank, l=lhs_list[k], r_=rhs_list[k], st=(k == 0), sp_=(k == n - 1), M=M, nc_=ncols:
                 e.matmul(ps[b][0:M, 0:nc_], lhsT=l, rhs=r_, start=st, stop=sp_), r=rkeys, w=[("ps", bank)])

    def rope_pair(b1, b2, cosT, sinT, M, out1, out2, k1, k2, tabkeys):
        P.op("dve", lambda e: e.tensor_tensor(out=tmp[0][0:M, :], in0=ps[b1][0:M, :], in1=cosT[0:M, :], op=ALU.mult), r=[("ps", b1)] + tabkeys, w=["tmp0"])
        P.op("dve", lambda e: e.tensor_tensor(out=tmp[1][0:M, :], in0=ps[b2][0:M, :], in1=sinT[0:M, :], op=ALU.mult), r=[("ps", b2)] + tabkeys, w=["tmp1"])
        P.op("dve", lambda e: e.tensor_tensor(out=tmp[2][0:M, :], in0=ps[b2][0:M, :], in1=cosT[0:M, :], op=ALU.mult), r=[("ps", b2)] + tabkeys, w=["tmp2"])
        P.op("dve", lambda e: e.tensor_tensor(out=tmp[3][0:M, :], in0=ps[b1][0:M, :], in1=sinT[0:M, :], op=ALU.mult), r=[("ps", b1)] + tabkeys, w=["tmp3"])
        P.op("dve", lambda e: e.tensor_tensor(out=out1, in0=tmp[0][0:M, :], in1=tmp[1][0:M, :], op=ALU.subtract), r=["tmp0", "tmp1"], w=[k1])
        P.op("dve", lambda e: e.tensor_tensor(out=out2, in0=tmp[2][0:M, :], in1=tmp[3][0:M, :], op=ALU.add), r=["tmp2", "tmp3"], w=[k2])

    def sin_table(dst, src_ang, shift, key, extra=()):
        a = src_ang
        if shift != 0.0:
            P.op("dve", lambda e: e.tensor_scalar(out=ang2, in0=src_ang, scalar1=float(shift), scalar2=None, op0=ALU.add), r=["angA"], w=["ang2"])
            a = ang2
        P.op("dve", lambda e, a=a: e.tensor_scalar(out=ki, in0=a, scalar1=float(1.0 / TWO_PI), scalar2=None, op0=ALU.mult), r=["angA", "ang2"], w=["ki"])
        P.op("dve", lambda e: e.tensor_copy(out=kf, in_=ki), r=["ki"], w=["kf"])
        P.op("dve", lambda e, a=a: e.scalar_tensor_tensor(out=rr, in0=kf, scalar=-C1, in1=a, op0=ALU.mult, op1=ALU.add), r=["kf", "angA", "ang2"], w=["rr"])
        P.op("dve", lambda e: e.scalar_tensor_tensor(out=rr, in0=kf, scalar=-C2, in1=rr, op0=ALU.mult, op1=ALU.add), r=["kf", "rr"], w=["rr"])
        P.op("act", lambda e: e.activation(out=dst, in_=rr, func=AF.Sin), r=["rr"], w=[key] + list(extra))

    def store(dst, src, rkeys, wkeys_):
        P.dma("pool", lambda e, a=dst, b=src: e.dma_start(out=a, in_=b), r=rkeys, w=wkeys_)

    TABS = [(COSD, SIND, COSM, SINM),
            (WS[0][0][:, 0:512], WS[0][0][:, 512:1024], WS[0][1][:, 0:512], WS[0][1][:, 512:1024])]

    def build_tables(s, g, tset):
        tsl_ = slice(g * 512, g * 512 + 512)
        cd, sd, cm, sm_ = TABS[tset]
        extra = [] if tset == 0 else [("ws", 0), ("ws", 1)]
        P.dma("sp", lambda e: e.dma_start(out=posi, in_=pos[s:s + 1, tsl_].broadcast_to([128, 512])), w=["posi"])
        P.op("dve", lambda e: e.tensor_copy(out=posf, in_=posi), r=["posi"], w=["posf"])
        for (col, ct, st_, kc, ks) in ((INVFD, cd, sd, "cd", "sd"), (INVFM, cm, sm_, "cm", "sm")):
            P.op("dve", lambda e, col=col: e.tensor_scalar(out=angA, in0=posf, scalar1=sm[:, col:col + 1], scalar2=None, op0=ALU.mult),
                 r=["posf", "small"], w=["angA"])
            sin_table(ct, angA, math.pi / 2.0, ("tab", kc, tset), extra)
            sin_table(st_, angA, 0.0, ("tab", ks, tset), extra)

    for s in range(NSEQ):
        for g in range(NG):
            t0 = g * 512
            tsl = slice(t0, t0 + 512)
            gidx = s * NG + g
            tset = gidx % 2
            COSD, SIND, COSM, SINM = TABS[tset]
            kCOSD, kSIND, kCOSM, kSINM = (("tab", nm, tset) for nm in ("cd", "sd", "cm", "sm"))
            if gidx == 0:
                build_tables(s, g, 0)
            xsl = (s * NG + g) % 2
            xb = xbA[xsl]
            for k in range(8):
                slot = k % 4
                P.dma("sp", lambda e, s=s, k=k, tsl=tsl, slot=slot: e.dma_start(out=xs[slot], in_=xT[s, k * 128:(k + 1) * 128, tsl]), w=[("xs", slot)])
                P.op("dve", lambda e, k=k, slot=slot, xb=xb: e.tensor_copy(out=xb[:, k, :], in_=xs[slot]), r=[("xs", slot)], w=[("xb", xsl, k)])
                store(xbs[s, k * 128:(k + 1) * 128, tsl], xb[:, k, :], [("xb", xsl, k)], [("xbs", s, g, k)])
            xbk = [("xb", xsl, k) for k in range(8)]
            def fm_chunk(c, M=128):
                b = next_bank()
                mm_group(b, M, [WA[:, k, c * 128:c * 128 + M] for k in range(8)], [xb[:, k, :] for k in range(8)], xbk + kWA)
                return b
            b1 = fm_chunk(0)
            b2 = fm_chunk(1)
            i1, i2 = next_stg(), next_stg()
            rope_pair(b1, b2, COSD, SIND, 128, stg[i1], stg[i2], ("stg", i1), ("stg", i2), [kCOSD, kSIND])
            store(fmd[s, 0:128, tsl], stg[i1], [("stg", i1)], [("fmd", s, g, 0)])
            store(fmd[s, 128:256, tsl], stg[i2], [("stg", i2)], [("fmd", s, g, 1)])
            for (c0, nch, fbuf, nbuf, nm, dim) in ((8, 3, cqf, cqn, "cq", 384.0), (11, 2, ckvf, ckvn, "ckv", 256.0)):
                sqs = []
                for j in range(nch):
                    b = fm_chunk(c0 + j)
                    qi = sq_rr[0]; sq_rr[0] = (qi + 1) % 2
                    P.op("act", lambda e, b=b, qi=qi: e.activation(out=sqb[qi], in_=ps[b], func=AF.Square), r=[("ps", b)], w=[("sqb", qi)])
                    P.op("act", lambda e, b=b, j=j, fbuf=fbuf: e.activation(out=fbuf[:, j, :], in_=ps[b], func=AF.Copy), r=[("ps", b)], w=[(nm + "f", j)])
                    sqs.append(qi)
                    if j == 0:
                        bs = next_bank()
                    P.op("pe", lambda e, bs=bs, qi=qi, st=(j == 0), sp_=(j == nch - 1): e.matmul(ps[bs], lhsT=ones, rhs=sqb[qi], start=st, stop=sp_),
                         r=[("sqb", qi), "ones"], w=[("ps", bs)])
                P.op("act", lambda e, bs=bs, dim=dim: e.activation(out=lnt, in_=ps[bs], func=AF.Ln, scale=1.0 / dim, bias=1e-6), r=[("ps", bs)], w=["lnt"])
                P.op("act", lambda e: e.activation(out=rstd, in_=lnt, func=AF.Exp, scale=-0.5), r=["lnt"], w=["rstd"])
                for j in range(nch):
                    P.op("dve", lambda e, j=j, fbuf=fbuf, nbuf=nbuf: e.tensor_tensor(out=nbuf[:, j, :], in0=fbuf[:, j, :], in1=rstd, op=ALU.mult),
                         r=[(nm + "f", j), "rstd"], w=[(nm + "n", j)])
            cqk = [("cqn", j) for j in range(3)]
            ckvk = [("ckvn", j) for j in range(2)]
            for c in range(2, 8):
                b = fm_chunk(c)
                i = next_stg()
                P.op("act", lambda e, b=b, i=i: e.activation(out=stg[i], in_=ps[b], func=AF.Copy), r=[("ps", b)], w=[("stg", i)])
                store(fmd[s, c * 128:(c + 1) * 128, tsl], stg[i], [("stg", i)], [("fmd", s, g, c)])
            for tt in range(4):
                b = next_bank()
                mm_group(b, 128, [xb[:, k, tt * 128:(tt + 1) * 128] for k in range(8)], [WV[:, k, :] for k in range(8)], xbk + kWV)
                P.op("act", lambda e, b=b, tt=tt: e.activation(out=vdt[:, tt, :], in_=ps[b], func=AF.Copy), r=[("ps", b)], w=[("vdt", tt)])
            store(vscr[s, 0, :, g * 2048:(g + 1) * 2048], vdt.rearrange("p a b -> p (a b)"), [("vdt", tt) for tt in range(4)], [("vscr", s, 0, g)])

            b = fm_chunk(13, M=48)
            i = next_stg()
            tk = [kCOSM, kSINM]
            P.op("dve", lambda e, b=b, COSM=COSM: e.tensor_tensor(out=tmp[0][0:16, :], in0=ps[b][0:16, :], in1=COSM[0:16, :], op=ALU.mult), r=[("ps", b)] + tk, w=["tmp0"])
            P.op("dve", lambda e, b=b, SINM=SINM: e.tensor_tensor(out=tmp[1][0:16, :], in0=SINM[0:16, :], in1=ps[b][32:48, :], op=ALU.mult), r=[("ps", b)] + tk, w=["tmp1"])
            P.op("dve", lambda e, b=b, COSM=COSM: e.tensor_tensor(out=tmp[2][32:48, :], in0=ps[b][32:48, :], in1=COSM[32:48, :], op=ALU.mult), r=[("ps", b)] + tk, w=["tmp2"])
            P.op("dve", lambda e, b=b, SINM=SINM: e.tensor_tensor(out=tmp[3][32:48, :], in0=SINM[32:48, :], in1=ps[b][0:16, :], op=ALU.mult), r=[("ps", b)] + tk, w=["tmp3"])
            P.op("dve", lambda e, i=i: e.tensor_tensor(out=stg[i][0:16, :], in0=tmp[0][0:16, :], in1=tmp[1][0:16, :], op=ALU.subtract), r=["tmp0", "tmp1"], w=[("stg", i)])
            P.op("dve", lambda e, i=i: e.tensor_tensor(out=stg[i][32:48, :], in0=tmp[2][32:48, :], in1=tmp[3][32:48, :], op=ALU.add), r=["tmp2", "tmp3", ("stg", i)], w=[("stg", i)])
            store(fmm[s, 1280:1296, tsl], stg[i][0:16, :], [("stg", i)], [("fmm", s, g, "kr1")])
            store(fmm[s, 1296:1312, tsl], stg[i][32:48, :], [("stg", i)], [("fmm", s, g, "kr2")])
            def q_chunk(qc):
                b = next_bank()
                mm_group(b, 128, [WUQ[:, k, qc * 128:(qc + 1) * 128] for k in range(3)], [cqn[:, k, :] for k in range(3)], cqk + kWUQ)
                return b
            b1 = q_chunk(0)
            b2 = q_chunk(1)
            i1, i2 = next_stg(), next_stg()
            rope_pair(b1, b2, COSM, SINM, 128, stg[i1], stg[i2], ("stg", i1), ("stg", i2), tk)
            store(fmm[s, 0:128, tsl], stg[i1], [("stg", i1)], [("fmm", s, g, 0)])
            store(fmm[s, 128:256, tsl], stg[i2], [("stg", i2)], [("fmm", s, g, 1)])
            for qc in range(2, 6):
                b = q_chunk(qc)
                i = next_stg()
                P.op("act", lambda e, b=b, i=i: e.activation(out=stg[i], in_=ps[b], func=AF.Copy), r=[("ps", b)], w=[("stg", i)])
                store(fmm[s, qc * 128:(qc + 1) * 128, tsl], stg[i], [("stg", i)], [("fmm", s, g, qc)])
            for kc in range(4):
                b = next_bank()
                mm_group(b, 128, [WUKV[:, k, kc * 128:(kc + 1) * 128] for k in range(2)], [ckvn[:, k, :] for k in range(2)], ckvk + kWUKV)
                i = next_stg()
                P.op("act", lambda e, b=b, i=i: e.activation(out=stg[i], in_=ps[b], func=AF.Copy), r=[("ps", b)], w=[("stg", i)])
                store(fmm[s, 768 + kc * 128:768 + (kc + 1) * 128, tsl], stg[i], [("stg", i)], [("fmm", s, g, 6 + kc)])
            for tt in range(4):
                b = next_bank()
                mm_group(b, 128, [ckvn[:, k, tt * 128:(tt + 1) * 128] for k in range(2)], [WUKV[:, k, 512:1024] for k in range(2)], ckvk + kWUKV)
                P.op("dve", lambda e, b=b, tt=tt: e.tensor_copy(out=vmt[:, tt, :], in_=ps[b]), r=[("ps", b)], w=[("vmt", tt)])
            store(vscr[s, 1, :, g * 2048:(g + 1) * 2048], vmt.rearrange("p a b -> p (a b)"), [("vmt", tt) for tt in range(4)], [("vscr", s, 1, g)])
            if gidx + 1 < NSEQ * NG:
                build_tables((gidx + 1) // NG, (gidx + 1) % NG, (gidx + 1) % 2)
    P.barrier()

    ABF.reset(); AFF.reset()
    WS[0] = [AFF.take(1024) for _ in range(2)]
    VDs = [v3(ABF.take(32 * 128), 32, 128) for _ in range(2)]
    VM = ABF.take(32 * 8 * 128).rearrange("p (t h d) -> p t h d", t=32, h=8, d=128)
    QT = [v3(ABF.take(2 * S), 2, S) for _ in range(2)]
    KT = [ABF.take(S) for _ in range(2)]
    NPT = 4
    PTv = [AFF.take(512).bitcast(BF16).rearrange("p (a b) -> p a b", a=2, b=512) for _ in range(NPT)]
    ostg = [ABF.take(512) for _ in range(2)]
    sqeA = [ABF.take(512) for _ in range(2)]
    EA = [[AFF.take(512) for _ in range(5)] for _ in range(2)]
    Em = [AFF.take(512) for _ in range(2)]
    P.op("pool", lambda e: e.memset(VM.rearrange("p t h d -> p (t h d)"), 1.0), w=["VM"])
    sc_rr = [0]
    pt_rr = [0]
    og_rr = [0]
    acc_rr = [0]

    def load_qk(s, kind, h, slot):
        Q, K = QT[slot], KT[slot]
        if kind == "d":
            if h < 2:
                P.op("pool", lambda e, Q=Q: e.memset(Q[64:128, 0, :], 0.0), w=[("Q", slot)])
                P.op("pool", lambda e, Q=Q: e.memset(Q[0:64, 1, :], 0.0), w=[("Q", slot)])
            P.dma("sp", lambda e, h=h: e.dma_start(out=VDs[slot], in_=vscr[s, 0, :, :].rearrange("p (t c) -> p t c", t=32, c=512)[:, :, h * 128:(h + 1) * 128]),
                  w=[("VD", slot)])
            for m in range(2):
                j = 2 * h + m
                for (dst0, n, qrow, krow) in ((0, 8, j * 8, 64 + j * 8), (8, 8, 128 + j * 8, 192 + j * 8), (16, 48, 256 + j * 48, 640 + j * 48)):
                    p0 = m * 64 + dst0
                    P.dma("sp", lambda e, p0=p0, n=n, qrow=qrow, Q=Q, m=m: e.dma_start(out=Q[p0:p0 + n, m, :], in_=fmd[s, qrow:qrow + n, :]), w=[("Q", slot)])
                    P.dma("sp", lambda e, p0=p0, n=n, krow=krow, K=K: e.dma_start(out=K[p0:p0 + n, :], in_=fmd[s, krow:krow + n, :]), w=[("K", slot)])
        else:
            for (p0, n, qrow, krow) in ((0, 64, 256 + h * 64, 768 + h * 64), (64, 16, h * 16, 1280), (80, 16, 128 + h * 16, 1296)):
                P.dma("sp", lambda e, p0=p0, n=n, qrow=qrow, Q=Q: e.dma_start(out=Q[p0:p0 + n, 0, :], in_=fmm[s, qrow:qrow + n, :]), w=[("Q", slot)])
                P.dma("sp", lambda e, p0=p0, n=n, krow=krow, K=K: e.dma_start(out=K[p0:p0 + n, :], in_=fmm[s, krow:krow + n, :]), w=[("K", slot)])

    pair_rr = [0]
    ep_rr = [0]
    pending_ep = []

    def qk_pair(slot, specs, scale, npairs):
        K = KT[slot]
        pi = pair_rr[0] % npairs
        pair_rr[0] += 1
        b0 = 2 * pi
        c0s = []
        for idx, (qm, kd, kb, qg) in enumerate(specs):
            b = b0 + idx
            Q = QT[slot][:, qm, :]
            j = kb - 4 * qg
            c0 = 128 * j if j >= 0 else 0
            P.op("pe", lambda e, b=b, c0=c0, kd=kd, kb=kb, qg=qg, Q=Q, j=j: e.matmul(
                ps[b][:, c0:512], lhsT=K[0:kd, kb * 128:(kb + 1) * 128], rhs=Q[0:kd, qg * 512 + c0:(qg + 1) * 512], start=True, stop=(j < 0)),
                r=[("Q", slot), ("K", slot)], w=[("ps", b)])
            if j >= 0:
                P.op("pe", lambda e, b=b, c0=c0: e.matmul(ps[b][:, c0:c0 + 128], lhsT=ident, rhs=maskb, start=False, stop=True), r=["cbf"], w=[("ps", b)])
            c0s.append(c0)
        cu = min(c0s)
        pt = pt_rr[0]; pt_rr[0] = (pt + 1) % NPT
        P.op("act", lambda e: e.activation(out=PTv[pt][:, :, cu:512], in_=pspair(b0)[:, :, cu:512], func=AF.Exp, scale=scale),
             r=[("ps", b0), ("ps", b0 + 1)], w=[("PT", pt)])
        return pt, c0s

    def flush_ep():
        while pending_ep:
            pending_ep.pop(0)()

    def attn_diff(s, h, qg, slot):
        A = [4, 5]
        R = [6, 7]
        nkb = 4 * qg + 4
        pend = []

        def pv(kb, pt, c0s):
            last = (kb == nkb - 1)
            for m in range(2):
                c0 = c0s[m]
                P.op("pe", lambda e, m=m, c0=c0: e.matmul(ps[A[m]][:, c0:512], lhsT=VDs[slot][:, kb, :], rhs=PTv[pt][:, m, c0:512], start=(kb == 0), stop=last),
                     r=[("PT", pt), ("VD", slot)], w=[("ps", A[m])])
                P.op("pe", lambda e, m=m, c0=c0: e.matmul(ps[R[m]][:, c0:512], lhsT=ones, rhs=PTv[pt][:, m, c0:512], start=(kb == 0), stop=last),
                     r=[("PT", pt), "ones"], w=[("ps", R[m])])

        for kb in range(nkb):
            pt, c0s = qk_pair(slot, [(0, 128, kb, qg), (1, 128, kb, qg)], 0.125, 2)
            pend.append((kb, pt, c0s))
            if len(pend) > 1:
                pv(*pend.pop(0))
            if kb == 2:
                flush_ep()
        while pend:
            pv(*pend.pop(0))
        flush_ep()
        ei = ep_rr[0]; ep_rr[0] = (ei + 1) % 2
        E = EA[ei]
        sqe = sqeA[ei]
        ek = [("E", ei, i) for i in range(5)]
        P.op("act", lambda e: e.activation(out=E[2], in_=ps[A[0]], func=AF.Copy), r=[("ps", A[0])], w=[ek[2]])
        P.op("dve", lambda e: e.reciprocal(out=E[0], in_=ps[R[0]]), r=[("ps", R[0])], w=[ek[0]])
        P.op("act", lambda e: e.activation(out=E[3], in_=ps[A[1]], func=AF.Copy), r=[("ps", A[1])], w=[ek[3]])
        P.op("dve", lambda e: e.reciprocal(out=E[1], in_=ps[R[1]]), r=[("ps", R[1])], w=[ek[1]])
        P.op("dve", lambda e: e.tensor_tensor(out=E[2], in0=E[2], in1=E[0], op=ALU.mult), r=[ek[0], ek[2]], w=[ek[2]])
        P.op("dve", lambda e: e.tensor_tensor(out=E[3], in0=E[3], in1=E[1], op=ALU.mult), r=[ek[1], ek[3]], w=[ek[3]])
        P.op("dve", lambda e: e.scalar_tensor_tensor(out=E[4], in0=E[3], scalar=lw[:, NEGLAM:NEGLAM + 1], in1=E[2], op0=ALU.mult, op1=ALU.add),
             r=[ek[2], ek[3], "neglam"], w=[ek[4]])

        def part2():
            P.op("act", lambda e: e.activation(out=sqe, in_=E[4], func=AF.Square), r=[ek[4]], w=[("sqe", ei)])
            pi = pair_rr[0] % 2
            pair_rr[0] += 1
            bm = 2 * pi
            P.op("pe", lambda e: e.matmul(ps[bm], lhsT=ones, rhs=sqe, start=True, stop=True), r=[("sqe", ei), "ones"], w=[("ps", bm)])
            P.op("act", lambda e: e.activation(out=E[0], in_=ps[bm], func=AF.Ln, scale=1.0 / 128.0, bias=1e-6), r=[("ps", bm)], w=[ek[0]])
            P.op("act", lambda e: e.activation(out=E[1], in_=E[0], func=AF.Exp, scale=-0.5), r=[ek[0]], w=[ek[1]])
            og = og_rr[0]; og_rr[0] = (og + 1) % 2
            P.op("dve", lambda e: e.scalar_tensor_tensor(out=ostg[og], in0=E[4], scalar=lw[:, SUBGS:SUBGS + 1], in1=E[1], op0=ALU.mult, op1=ALU.mult),
                 r=[ek[4], ek[1], "subgs"], w=[("ostg", og)])
            store(oTs[s, h * 128:(h + 1) * 128, qg * 512:(qg + 1) * 512], ostg[og], [("ostg", og)], [("oTs", s, h, qg)])

        pending_ep.append(part2)

    def attn_mla(s, h, qg, slot):
        flush_ep()
        a = 6 + acc_rr[0]; acc_rr[0] = (acc_rr[0] + 1) % 2
        nkb = 4 * qg + 4
        pend = []
        scale = 96.0 ** -0.5

        def pv(kb0, pt, c0s):
            for idx in range(2):
                kb = kb0 + idx
                c0 = c0s[idx]
                P.op("pe", lambda e, idx=idx, kb=kb, c0=c0: e.matmul(ps[a][:, c0:512], lhsT=VM[:, kb, h, :], rhs=PTv[pt][:, idx, c0:512],
                                                                      start=(kb == 0), stop=(kb == nkb - 1)),
                     r=[("PT", pt), "VM"], w=[("ps", a)])

        for kb0 in range(0, nkb, 2):
            pt, c0s = qk_pair(slot, [(0, 96, kb0, qg), (0, 96, kb0 + 1, qg)], scale, 3)
            pend.append((kb0, pt, c0s))
            if len(pend) > 2:
                pv(*pend.pop(0))
        while pend:
            pv(*pend.pop(0))
        ei = a % 2
        P.op("dve", lambda e: e.reciprocal(out=Em[ei][64:128, :], in_=ps[a][64:128, :]), r=[("ps", a)], w=[("Em", ei)])
        og = og_rr[0]; og_rr[0] = (og + 1) % 2
        P.op("dve", lambda e: e.tensor_tensor(out=ostg[og][0:64, :], in0=ps[a][0:64, :], in1=Em[ei][64:128, :], op=ALU.mult),
             r=[("ps", a), ("Em", ei)], w=[("ostg", og)])
        store(oTs[s, 512 + h * 64:512 + (h + 1) * 64, qg * 512:(qg + 1) * 512], ostg[og][0:64, :], [("ostg", og)], [("oTs", s, 4 + h, qg)])

    jobs = []
    for s in range(NSEQ):
        for h in range(4):
            jobs.append((s, "d", h))
        for h in range(8):
            jobs.append((s, "m", h))
    for ji, (s, kind, h) in enumerate(jobs):
        slot = ji % 2
        if ji == 0:
            load_qk(s, kind, h, slot)
        if kind == "d" and h == 0:
            P.dma("sp", lambda e, s=s: e.dma_start(out=VM[:, :, :, 0:64], in_=vscr[s, 1, :, :].rearrange("p (t h d) -> p t h d", t=32, h=8, d=64)), w=["VM"])
        if ji + 1 < len(jobs):
            load_qk(*jobs[ji + 1], (ji + 1) % 2)
        for qg in range(NG):
            if kind == "d":
                attn_diff(s, h, qg, slot)
            else:
                attn_mla(s, h, qg, slot)
    flush_ep()

    P.barrier()

    ABF.reset(); AFF.reset()
    WS[0] = [AFF.take(1024) for _ in range(2)]
    WG = v3(ABF.take(8 * 2048), 8, 2048)
    WOD = v3(ABF.take(4 * 1024), 4, 1024)
    WOM = v3(ABF.take(4 * 1024), 4, 1024)
    WOUT = v3(ABF.take(8 * 1024), 8, 1024)
    load_cast(WG, wg, 8, 2048, tag="WG")
    load_cast(WOD, wod, 4, 1024, tag="WOD")
    load_cast(WOM, wom, 4, 1024, tag="WOM")
    load_cast(WOUT, wout, 8, 1024, tag="WOUT")
    kWG = wkeys("WG", 8, 2048); kWOD = wkeys("WOD", 4, 1024); kWOM = wkeys("WOM", 4, 1024); kWOUT = wkeys("WOUT", 8, 1024)
    xb2A = [v3(ABF.take(8 * 512), 8, 512) for _ in range(2)]
    oTgA = [v3(ABF.take(8 * 512), 8, 512) for _ in range(2)]
    yTA = [v3(ABF.take(8 * 512), 8, 512) for _ in range(2)]
    x1bA = [ABF.take(1024) for _ in range(2)]
    x1TA = [v3(ABF.take(8 * 128), 8, 128) for _ in range(2)]
    xtokA = [AFF.take(1024) for _ in range(5)]
    xt_rr = [0]
    G0, G1, TT_, UU_ = (AFF.take(512) for _ in range(4))
    rbufA = [AFF.take(1024) for _ in range(2)]
    x1fA = [AFF.take(1024) for _ in range(2)]
    lng = AFF.take(1024)
    lnb = AFF.take(1024)
    st6A = [stt[:, 0, :, :], stt[:, 1, :, :]]
    mvA = [mvt[:, 0, :], mvt[:, 1, :]]
    ln_rr = [0]

    def ln_stats(rin, rkey):
        li = ln_rr[0]; ln_rr[0] = (li + 1) % 2
        st6 = st6A[li]; mv = mvA[li]
        P.op("dve", lambda e: e.bn_stats(out=st6[:, 0, :], in_=rin[:, 0:512]), r=[rkey], w=[("st0", li)])
        P.op("dve", lambda e: e.bn_stats(out=st6[:, 1, :], in_=rin[:, 512:1024]), r=[rkey], w=[("st1", li)])
        P.op("dve", lambda e: e.bn_aggr(out=mv[:, 0:2], in_=st6), r=[("st0", li), ("st1", li)], w=[("mv", li)])
        P.op("act", lambda e: e.activation(out=mv[:, 2:3], in_=mv[:, 1:2], func=AF.Ln, bias=1e-5), r=[("mv", li)], w=[("mv2", li)])
        P.op("act", lambda e: e.activation(out=mv[:, 3:4], in_=mv[:, 2:3], func=AF.Exp, scale=-0.5), r=[("mv2", li)], w=[("mv3", li)])
        return li

    def layer_norm(rin, rkey, gam, bet, gkeys, out, okey, xnb=None, xnbkey=None):
        li = ln_stats(rin, rkey)
        mv = mvA[li]
        if xnb is not None:
            P.op("dve", lambda e: e.tensor_scalar(out=xnb, in0=rin, scalar1=mv[:, 0:1], scalar2=mv[:, 3:4], op0=ALU.subtract, op1=ALU.mult),
                 r=[rkey, ("mv", li), ("mv3", li)], w=[xnbkey])
        P.op("dve", lambda e: e.tensor_scalar(out=rin, in0=rin, scalar1=mv[:, 0:1], scalar2=mv[:, 3:4], op0=ALU.subtract, op1=ALU.mult),
             r=[rkey, ("mv", li), ("mv3", li)], w=[rkey])
        P.op("dve", lambda e: e.tensor_tensor(out=rin, in0=rin, in1=gam, op=ALU.mult), r=[rkey] + gkeys, w=[rkey])
        P.op("dve", lambda e: e.tensor_tensor(out=out, in0=rin, in1=bet, op=ALU.add), r=[rkey] + gkeys, w=[okey])

    P.dma("sp", lambda e: e.dma_start(out=lng, in_=lnp[0:1, :].broadcast_to([128, 1024])), w=["lng"])
    P.dma("sp", lambda e: e.dma_start(out=lnb, in_=lnp[1:2, :].broadcast_to([128, 1024])), w=["lnb"])

    def c_loads(s, g):
        t0 = g * 512
        tsl = slice(t0, t0 + 512)
        gsl = (s * NG + g) % 2
        ctx = dict(s=s, g=g, t0=t0, gsl=gsl, xb2=xb2A[gsl], oTg=oTgA[gsl], yT=yTA[gsl], kxb2=("xb2", gsl), koTg=("oTg", gsl), xts=[])
        P.dma("sp", lambda e, xb2=ctx["xb2"]: e.dma_start(out=xb2, in_=xbs[s, :, tsl].rearrange("(k p) t -> p k t", p=128)), w=[ctx["kxb2"]])
        P.dma("sp", lambda e, oTg=ctx["oTg"]: e.dma_start(out=oTg, in_=oTs[s, :, tsl].rearrange("(k p) t -> p k t", p=128)), w=[ctx["koTg"]])
        return ctx

    def c_fchunk(ctx, f):
        xb2, oTg, yT, kxb2, koTg, gsl = ctx["xb2"], ctx["oTg"], ctx["yT"], ctx["kxb2"], ctx["koTg"], ctx["gsl"]
        ba = next_bank()
        mm_group(ba, 128, [WG[:, k, f * 128:(f + 1) * 128] for k in range(8)], [xb2[:, k, :] for k in range(8)], [kxb2] + kWG)
        bb = next_bank()
        mm_group(bb, 128, [WG[:, k, 1024 + f * 128:1024 + (f + 1) * 128] for k in range(8)], [xb2[:, k, :] for k in range(8)], [kxb2] + kWG)
        bc = next_bank()
        mm_group(bc, 128, [WOD[:, k, f * 128:(f + 1) * 128] for k in range(4)], [oTg[:, k, :] for k in range(4)], [koTg] + kWOD)
        bd = next_bank()
        mm_group(bd, 128, [WOM[:, k, f * 128:(f + 1) * 128] for k in range(4)], [oTg[:, 4 + k, :] for k in range(4)], [koTg] + kWOM)
        P.op("act", lambda e: e.activation(out=G0, in_=ps[ba], func=AF.Sigmoid, bias=sm[:, GATEB + f:GATEB + f + 1]), r=[("ps", ba), "small"], w=["G0"])
        P.op("act", lambda e: e.activation(out=G1, in_=ps[bb], func=AF.Sigmoid, bias=sm[:, GATEB + 8 + f:GATEB + 9 + f]), r=[("ps", bb), "small"], w=["G1"])
        P.op("dve", lambda e: e.tensor_tensor(out=TT_, in0=ps[bc], in1=G0, op=ALU.mult), r=[("ps", bc), "G0"], w=["TT"])
        P.op("dve", lambda e: e.tensor_tensor(out=UU_, in0=ps[bd], in1=G1, op=ALU.mult), r=[("ps", bd), "G1"], w=["UU"])
        P.op("dve", lambda e: e.tensor_tensor(out=yT[:, f, :], in0=TT_, in1=UU_, op=ALU.add), r=["TT", "UU"], w=[("yT", gsl, f)])

    tile_rr = [0]

    def c_tile_a(ctx, tt):
        s, t0, yT, gsl = ctx["s"], ctx["t0"], ctx["yT"], ctx["gsl"]
        ti = tile_rr[0]; tile_rr[0] = (ti + 1) % 2
        xi = xt_rr[0]; xt_rr[0] = (xi + 1) % 5
        P.dma("sp", lambda e: e.dma_start(out=xtokA[xi], in_=x[s, t0 + tt * 128:t0 + (tt + 1) * 128, :]), w=[("xtok", xi)])
        rbuf, x1f, x1b = rbufA[ti], x1fA[ti], x1bA[ti]
        yk = [("yT", gsl, f) for f in range(8)]
        for n in range(2):
            b = next_bank()
            mm_group(b, 128, [yT[:, k, tt * 128:(tt + 1) * 128] for k in range(8)], [WOUT[:, k, n * 512:(n + 1) * 512] for k in range(8)], yk + kWOUT)
            P.op("dve", lambda e, b=b, n=n: e.scalar_tensor_tensor(out=rbuf[:, n * 512:(n + 1) * 512], in0=xtokA[xi][:, n * 512:(n + 1) * 512],
                                                                  scalar=ALPHA, in1=ps[b], op0=ALU.mult, op1=ALU.add),
                 r=[("ps", b), ("xtok", xi)], w=[("rbuf", ti)])
        layer_norm(rbuf, ("rbuf", ti), lng, lnb, ["lng", "lnb"], x1f, ("x1f", ti), xnb=x1b, xnbkey=("x1b", ti))
        row0 = s * S + t0 + tt * 128
        store(x1s[row0:row0 + 128, :], x1f, [("x1f", ti)], [("x1s", row0)])
        return ti

    def c_tile_b(ctx, tt, ti):
        s, t0 = ctx["s"], ctx["t0"]
        x1b, x1T = x1bA[ti], x1TA[ti]
        b = next_bank()
        psb = ps[b].bitcast(BF16)
        for k in range(8):
            P.op("pe", lambda e, k=k: e.transpose(psb[:, k * 128:(k + 1) * 128], x1b[:, k * 128:(k + 1) * 128], ident),
                 r=[("x1b", ti), "cbf"], w=[("ps", b)])
        for k in range(8):
            P.op("act", lambda e, k=k: e.activation(out=x1T[:, k, :], in_=psb[:, k * 128:(k + 1) * 128], func=AF.Identity,
                                                    scale=sm[:, LN1G + k:LN1G + k + 1], bias=sm[:, LN1B + k:LN1B + k + 1]),
                 r=[("ps", b), "small"], w=[("x1T", ti)])
        c0 = s * S + t0 + tt * 128
        P.dma("sp", lambda e: e.dma_start(out=x1Ts[:, c0:c0 + 128].rearrange("(k p) t -> p k t", p=128), in_=x1T), r=[("x1T", ti)], w=[("x1Ts", c0)])

    cgroups = [(s, g) for s in range(NSEQ) for g in range(NG)]
    prev = None
    for gi in range(len(cgroups) + 1):
        cur = c_loads(*cgroups[gi]) if gi < len(cgroups) else None
        for i in range(4):
            ti = None
            if prev is not None:
                ti = c_tile_a(prev, i)
            if cur is not None:
                c_fchunk(cur, 2 * i)
                c_fchunk(cur, 2 * i + 1)
            if prev is not None:
                c_tile_b(prev, i, ti)
        prev = cur

    P.barrier()

    ABF.reset(); AFF.reset()
    WS[0] = [AFF.take(1024) for _ in range(2)]
    WUP = v3(ABF.take(8 * 4096), 8, 4096)
    WDN = v3(ABF.take(32 * 1024), 32, 1024)
    load_cast(WUP, wup, 8, 4096, tag="WUP")
    load_cast(WDN, wdn, 32, 1024, tag="WDN")
    kWUP = wkeys("WUP", 8, 4096); kWDN = wkeys("WDN", 32, 1024)
    x1TgA = [v3(ABF.take(8 * 256), 8, 256) for _ in range(2)]
    hT = [AFF.take(128).bitcast(BF16) for _ in range(3)]
    x1gA = [v3(AFF.take(2 * 1024), 2, 1024) for _ in range(2)]
    sqv = [AFF.take(256) for _ in range(2)]
    rb2A = [AFF.take(1024) for _ in range(2)]
    ob = [AFF.take(1024) for _ in range(2)]
    lng2 = AFF.take(1024)
    lnb2 = AFF.take(1024)
    P.dma("sp", lambda e: e.dma_start(out=lng2, in_=lnp[2:3, :].broadcast_to([128, 1024])), w=["lng2"])
    P.dma("sp", lambda e: e.dma_start(out=lnb2, in_=lnp[3:4, :].broadcast_to([128, 1024])), w=["lnb2"])
    ub_rr = [0]
    for gi in range(NSEQ * S // 256):
        r0 = gi * 256
        x1Tg = x1TgA[gi % 2]
        x1g = x1gA[gi % 2]
        kx1T = ("x1Tg", gi % 2)
        kx1g = ("x1g", gi % 2)
        P.dma("sp", lambda e, r0=r0, x1Tg=x1Tg: e.dma_start(out=x1Tg, in_=x1Ts[:, r0:r0 + 256].rearrange("(k p) t -> p k t", p=128)), w=[kx1T])
        P.dma("sp", lambda e, r0=r0, x1g=x1g: e.dma_start(out=x1g, in_=x1s[r0:r0 + 256, :].rearrange("(a p) d -> p a d", p=128)), w=[kx1g])
        pend = []

        def down(fc, hi):
            for tt in range(2):
                for n in range(2):
                    a = 4 + tt * 2 + n
                    P.op("pe", lambda e, a=a, tt=tt, n=n, fc=fc, hi=hi: e.matmul(ps[a], lhsT=hT[hi][:, tt * 128:(tt + 1) * 128], rhs=WDN[:, fc, n * 512:(n + 1) * 512],
                                                                               start=(fc == 0), stop=(fc == 31)),
                         r=[("hT", hi)] + kWDN, w=[("ps", a)])

        for fc in range(32):
            ub = ub_rr[0]; ub_rr[0] = (ub + 1) % 4
            mm_group(ub, 128, [WUP[:, k, fc * 128:(fc + 1) * 128] for k in range(8)], [x1Tg[:, k, :] for k in range(8)], [kx1T] + kWUP, ncols=256)
            qi = fc % 2
            hi = fc % 3
            P.op("act", lambda e, ub=ub, qi=qi: e.activation(out=sqv[qi], in_=ps[ub][:, 0:256], func=AF.Square), r=[("ps", ub)], w=[("sqv", qi)])
            P.op("dve", lambda e, ub=ub, qi=qi, hi=hi: e.scalar_tensor_tensor(out=hT[hi], in0=ps[ub][:, 0:256], scalar=0.0, in1=sqv[qi], op0=ALU.is_gt, op1=ALU.mult),
                 r=[("ps", ub), ("sqv", qi)], w=[("hT", hi)])
            pend.append((fc, hi))
            if len(pend) > 1:
                down(*pend.pop(0))
        while pend:
            down(*pend.pop(0))
        for tt in range(2):
            for n in range(2):
                a = 4 + tt * 2 + n
                P.op("dve", lambda e, a=a, tt=tt, n=n, x1g=x1g: e.scalar_tensor_tensor(out=rb2A[tt][:, n * 512:(n + 1) * 512], in0=x1g[:, tt, n * 512:(n + 1) * 512],
                                                                             scalar=ALPHA, in1=ps[a], op0=ALU.mult, op1=ALU.add),
                     r=[("ps", a), kx1g], w=[("rb2", tt)])
        for tt in range(2):
            layer_norm(rb2A[tt], ("rb2", tt), lng2, lnb2, ["lng2", "lnb2"], ob[tt], ("ob", tt))
            store(y[r0 + tt * 128:r0 + (tt + 1) * 128, :], ob[tt], [("ob", tt)], [("y", r0, tt)])

    P.barrier()

    with nc.Block() as block:
        semh = {}
        import contextlib
        with contextlib.ExitStack() as es:
            for e_ in ("pe", "act", "dve", "pool"):
                semh[e_] = es.enter_context(nc.semaphore("c_" + e_))
            for i in range(NDMA):
                semh[("dma", i)] = es.enter_context(nc.semaphore(f"d{i}"))

            @block.sync
            def _(e):
                P.emit("sp", e, semh)

            @block.tensor
            def _(e):
                P.emit("pe", e, semh)

            @block.scalar
            def _(e):
                P.emit("act", e, semh)

            @block.vector
            def _(e):
                P.emit("dve", e, semh)

            @block.gpsimd
            def _(e):
                P.emit("pool", e, semh)
    nc._prog_marks = P.marks
    return nc


def _chunked(w, C):
    K, N = w.shape
    assert K == C * 128
    return np.ascontiguousarray(w.reshape(C, 128, N).transpose(1, 0, 2).reshape(128, C * N))


_NC_CACHE = {}


def kernel(x, positions, w_in, gate_b, diff_lambda, diff_subln_g, mla_q_norm_g, w_uq,
           mla_kv_norm_g, w_ukv, w_o_diff, w_o_mla, w_out, ln1_g, ln1_b, w_up, w_down,
           ln2_g, ln2_b):
    f32 = np.float32
    x = np.asarray(x, f32)
    positions = np.asarray(positions, np.int32)
    w_in0 = np.asarray(w_in, f32)[0]
    j = np.arange(8)
    i8 = np.arange(8)
    dq_x1 = (64 * j[:, None] + i8[None, :]).reshape(-1)
    dq_x2 = dq_x1 + 8
    dq_np = (64 * j[:, None] + 16 + np.arange(48)[None, :]).reshape(-1)
    cols = np.concatenate([
        dq_x1, 512 + dq_x1,
        dq_x2, 512 + dq_x2,
        dq_np, 512 + dq_np,
        np.arange(1536, 1920),
        np.arange(1920, 2176),
        np.arange(2176, 2192), np.arange(2176, 2192), np.arange(2192, 2208),
    ])
    assert cols.size == NWA
    wa = _chunked(w_in0[:, cols], 8)
    wv = _chunked(w_in0[:, 1024:1536], 8)
    wg = _chunked(w_in0[:, 2208:4256], 8)
    h8 = np.arange(8)
    i16 = np.arange(16)
    uq_cols = np.concatenate([
        (96 * h8[:, None] + 64 + i16[None, :]).reshape(-1),
        (96 * h8[:, None] + 80 + i16[None, :]).reshape(-1),
        (96 * h8[:, None] + np.arange(64)[None, :]).reshape(-1),
    ])
    wuq = _chunked(np.asarray(w_uq, f32)[0][:, uq_cols], 3)
    ukv_cols = np.concatenate([
        (128 * h8[:, None] + np.arange(64)[None, :]).reshape(-1),
        (128 * h8[:, None] + 64 + np.arange(64)[None, :]).reshape(-1),
    ])
    wukv = _chunked(np.asarray(w_ukv, f32)[0][:, ukv_cols], 2)
    wod = _chunked(np.asarray(w_o_diff, f32)[0], 4)
    wom = _chunked(np.asarray(w_o_mla, f32)[0], 4)
    woutc = _chunked(np.asarray(w_out, f32)[0], 8)
    wupc = _chunked(np.asarray(w_up, f32)[0], 8)
    wdnc = _chunked(np.asarray(w_down, f32)[0], 32)
    small = np.zeros((128, 48), f32)
    small[:, 0:3] = np.asarray(mla_q_norm_g, f32)[0].reshape(3, 128).T
    small[:, 3:5] = np.asarray(mla_kv_norm_g, f32)[0].reshape(2, 128).T
    small[:, 5] = np.asarray(diff_subln_g, f32)[0]
    p = np.arange(128)
    small[:, 6] = np.power(np.float32(500000.0), -(p % 8).astype(f32) / np.float32(8)).astype(f32)
    small[:, 7] = np.power(np.float32(500000.0), -(p % 16).astype(f32) / np.float32(16)).astype(f32)
    gb = np.asarray(gate_b, f32)[0]
    small[:, 8:24] = gb.reshape(2, 8, 128).transpose(2, 0, 1).reshape(128, 16)
    small[:, 24:32] = np.asarray(ln1_g, f32)[0].reshape(8, 128).T
    small[:, 32:40] = np.asarray(ln1_b, f32)[0].reshape(8, 128).T
    lam = np.asarray(diff_lambda, f32)[0].reshape(1, 256)
    lnp = np.stack([np.asarray(a, f32)[0] for a in (ln1_g, ln1_b, ln2_g, ln2_b)], 0)
    cb = np.zeros((128, 256), f32)
    kk = np.arange(128)
    cb[:, 0:128] = np.where(kk[None, :] >= kk[:, None], 0.0, -30000.0)
    cb[:, 128:256] = np.eye(128)
    cbf = cb.astype(ml_dtypes.bfloat16)

    if "nc" not in _NC_CACHE:
        _NC_CACHE["nc"] = build()
    nc = _NC_CACHE["nc"]

    shared = dict(wa=wa, wv=wv, wg=wg, wuq=wuq, wukv=wukv, wod=wod, wom=wom, wout=woutc, wup=wupc, wdn=wdnc,
                  small=small, lam=lam, lnp=np.ascontiguousarray(lnp), cbf=cbf)
    in_maps = []
    for c in range(8):
        xc = np.ascontiguousarray(x[2 * c:2 * c + 2])
        m = dict(shared)
        m["x"] = xc
        m["xT"] = np.ascontiguousarray(xc.transpose(0, 2, 1))
        m["pos"] = np.ascontiguousarray(positions[2 * c:2 * c + 2])
        in_maps.append(m)
    res = run_bass_kernel_spmd(nc, in_maps, core_ids=list(range(8)))
    outs = [np.asarray(r["y"], f32).reshape(NSEQ, S, D) for r in res.results]
    return np.concatenate(outs, axis=0)
```

```python
import math
import numpy as np
import ml_dtypes
import concourse.bass as bass
import concourse.mybir as mybir
from concourse.bass_utils import run_bass_kernel_spmd

F32 = mybir.dt.float32
BF16 = mybir.dt.bfloat16
I32 = mybir.dt.int32
AF = mybir.ActivationFunctionType
ALU = mybir.AluOpType

S = 4096
D = 1024
NSEQ = 2
NG = 8
ALPHA = 2.0 ** 0.25
LAMBDA_INIT = 0.8 - 0.6 * math.exp(0.0)
NWA = 1712
NDMA = 32
TWO_PI = 2.0 * math.pi
C1 = 6.28125
C2 = TWO_PI - C1


class Prog:
    ENGS = ("pe", "act", "dve", "pool", "sp")

    def __init__(self):
        self.ops = {e: [] for e in self.ENGS}
        self.count = {e: 0 for e in self.ENGS}
        self.waited = {e: {} for e in self.ENGS}
        self.res_w = {}
        self.res_r = {}
        self.dma_val = [0] * NDMA
        self.dma_rr = 0
        self.marks = []

    def _deps(self, eng, r, w):
        need = {}

        def add(tok):
            if tok is None:
                return
            sk, v = tok
            if sk == "pe" and eng == "pe":
                return
            if v > need.get(sk, 0):
                need[sk] = v

        for k in r:
            add(self.res_w.get(k))
        for k in w:
            add(self.res_w.get(k))
            for sk, v in self.res_r.get(k, {}).items():
                add((sk, v))
        waits = []
        wd = self.waited[eng]
        for sk, v in need.items():
            if v > wd.get(sk, 0):
                wd[sk] = v
                waits.append((sk, v))
        return waits

    def _commit(self, tok, r, w):
        sk, v = tok
        for k in r:
            d = self.res_r.setdefault(k, {})
            if v > d.get(sk, 0):
                d[sk] = v
        for k in w:
            self.res_w[k] = tok
            self.res_r[k] = {}

    def op(self, eng, fn, r=(), w=()):
        waits = self._deps(eng, r, w)
        self.count[eng] += 1
        tok = (eng, self.count[eng])
        self._commit(tok, r, w)
        self.ops[eng].append((waits, fn, (eng, 1)))

    def dma(self, q, fn, r=(), w=()):
        waits = self._deps(q, r, w)
        i = self.dma_rr
        self.dma_rr = (i + 1) % NDMA
        sk = ("dma", i)
        prev = self.dma_val[i]
        wd = self.waited[q]
        if prev > wd.get(sk, 0):
            wd[sk] = prev
            waits.append((sk, prev))
        self.dma_val[i] = prev + 16
        tok = (sk, prev + 16)
        self._commit(tok, r, w)
        self.ops[q].append((waits, fn, (sk, 16)))

    def barrier(self):
        self.marks.append(dict(self.count))
        for e in self.ENGS:
            waits = []
            wd = self.waited[e]
            for o in ("pe", "act", "dve", "pool"):
                v = self.count[o]
                if o != e and v > wd.get(o, 0):
                    wd[o] = v
                    waits.append((o, v))
                if o == e and o != "pe" and v > wd.get(o, 0):
                    wd[o] = v
                    waits.append((o, v))
            for i in range(NDMA):
                sk = ("dma", i)
                v = self.dma_val[i]
                if v > wd.get(sk, 0):
                    wd[sk] = v
                    waits.append((sk, v))
            self.ops[e].append((waits, None, None))
        self.res_w = {}
        self.res_r = {}

    def emit(self, eng, e, sems):
        for waits, fn, inc in self.ops[eng]:
            for sk, v in waits:
                e.wait_ge(sems[sk], v)
            if fn is not None:
                ins = fn(e)
                ins.then_inc(sems[inc[0]], inc[1])


class Arena:
    def __init__(self, ap, n):
        self.ap = ap
        self.n = n
        self.off = 0

    def reset(self):
        self.off = 0

    def take(self, n):
        o = self.off
        self.off += n
        assert self.off <= self.n, (self.off, self.n)
        return self.ap[:, o:o + n]


def build():
    nc = bass.Bass("TRN2", target_bir_lowering=False)
    P = Prog()

    def din(name, shape, dt=F32):
        return nc.dram_tensor(name, shape, dt, kind="ExternalInput").ap()

    def dscr(name, shape, dt):
        return nc.dram_tensor(name, shape, dt, kind="Internal").ap()

    xT = din("xT", [NSEQ, D, S])
    x = din("x", [NSEQ, S, D])
    pos = din("pos", [NSEQ, S], I32)
    wa = din("wa", [128, 8 * NWA])
    wv = din("wv", [128, 8 * 512])
    wg = din("wg", [128, 8 * 2048])
    wuq = din("wuq", [128, 3 * 768])
    wukv = din("wukv", [128, 2 * 1024])
    wod = din("wod", [128, 4 * 1024])
    wom = din("wom", [128, 4 * 1024])
    wout = din("wout", [128, 8 * 1024])
    wup = din("wup", [128, 8 * 4096])
    wdn = din("wdn", [128, 32 * 1024])
    small = din("small", [128, 48])
    lam = din("lam", [1, 256])
    lnp = din("lnp", [4, 1024])
    cbf = din("cbf", [128, 256], BF16)
    y = nc.dram_tensor("y", [NSEQ * S, D], F32, kind="ExternalOutput").ap()

    fmd = dscr("fmd", [NSEQ, 1024, S], BF16)
    fmm = dscr("fmm", [NSEQ, 1312, S], BF16)
    vscr = dscr("vscr", [NSEQ, 2, 128, 32 * 512], BF16)
    xbs = dscr("xbs", [NSEQ, D, S], BF16)
    oTs = dscr("oTs", [NSEQ, D, S], BF16)
    x1s = dscr("x1s", [NSEQ * S, D], F32)
    x1Ts = dscr("x1Ts", [D, NSEQ * S], BF16)

    NBF = 69632
    NF = 15872
    abf_t = nc.alloc_sbuf_tensor("abf", [128, NBF], BF16)
    af_t = nc.alloc_sbuf_tensor("af", [128, NF], F32)
    smallt = nc.alloc_sbuf_tensor("smallt", [128, 48], F32)
    lamt = nc.alloc_sbuf_tensor("lamt", [128, 256], F32)
    lamw = nc.alloc_sbuf_tensor("lamw", [128, 8], F32)
    cbft = nc.alloc_sbuf_tensor("cbft", [128, 256], BF16)
    onest = nc.alloc_sbuf_tensor("onest", [128, 128], BF16)
    stt = nc.alloc_sbuf_tensor("stt", [128, 2, 2, 6], F32)
    mvt = nc.alloc_sbuf_tensor("mvt", [128, 2, 4], F32)
    psall_t = nc.alloc_psum_tensor("psall", [128, 4096], F32)
    psall = psall_t[:]
    ps = [psall[:, i * 512:(i + 1) * 512] for i in range(8)]

    def pspair(b0):
        return psall[:, b0 * 512:(b0 + 2) * 512].rearrange("p (a b) -> p a b", a=2, b=512)
    ABF = Arena(abf_t[:], NBF)
    AFF = Arena(af_t[:], NF)
    sm = smallt[:]
    maskb = cbft[:, 0:128]
    ident = cbft[:, 128:256]
    ones = onest[:]

    GQ, GKV, SUBG, INVFD, INVFM, GATEB, LN1G, LN1B = 0, 3, 5, 6, 7, 8, 24, 32
    NEGLAM, SUBGS = 0, 1

    P.dma("sp", lambda e: e.dma_start(out=sm, in_=small[:, :]), w=["small"])
    P.dma("sp", lambda e: e.dma_start(out=lamt[:], in_=lam[0:1, :].broadcast_to([128, 256])), w=["lamt"])
    P.dma("sp", lambda e: e.dma_start(out=cbft[:], in_=cbf[:, :]), w=["cbf"])
    P.op("pool", lambda e: e.memset(ones, 1.0), w=["ones"])
    lw = lamw[:]
    P.op("dve", lambda e: e.tensor_tensor(out=lamt[:, 0:64], in0=lamt[:, 0:64], in1=lamt[:, 64:128], op=ALU.mult), r=["lamt"], w=["lamA"])
    P.op("dve", lambda e: e.tensor_tensor(out=lamt[:, 128:192], in0=lamt[:, 128:192], in1=lamt[:, 192:256], op=ALU.mult), r=["lamt"], w=["lamB"])
    P.op("dve", lambda e: e.reduce_sum(out=lw[:, 2:3], in_=lamt[:, 0:64], axis=mybir.AxisListType.X), r=["lamA"], w=["lw2"])
    P.op("dve", lambda e: e.reduce_sum(out=lw[:, 3:4], in_=lamt[:, 128:192], axis=mybir.AxisListType.X), r=["lamB"], w=["lw3"])
    P.op("act", lambda e: e.activation(out=lw[:, 4:6], in_=lw[:, 2:4], func=AF.Exp), r=["lw2", "lw3"], w=["lw45"])
    P.op("dve", lambda e: e.tensor_tensor(out=lw[:, 6:7], in0=lw[:, 5:6], in1=lw[:, 4:5], op=ALU.subtract), r=["lw45"], w=["lw6"])
    P.op("dve", lambda e: e.tensor_scalar(out=lw[:, NEGLAM:NEGLAM + 1], in0=lw[:, 6:7], scalar1=-LAMBDA_INIT, scalar2=None, op0=ALU.add), r=["lw6"], w=["neglam"])
    P.op("dve", lambda e: e.tensor_scalar(out=lw[:, SUBGS:SUBGS + 1], in0=sm[:, SUBG:SUBG + 1], scalar1=1.0 - LAMBDA_INIT, scalar2=None, op0=ALU.mult), r=["small"], w=["subgs"])

    bank_rr = [0]
    WS = [None]

    lc_i = [0]

    def load_cast(dst3, src, C, N, scale_col=None, tag="w", c_list=None, o_list=None):
        ws = WS[0]
        c_list = list(range(C)) if c_list is None else list(c_list)
        o_list = list(range(0, N, 1024)) if o_list is None else list(o_list)
        for o in o_list:
            for c in c_list:
                n = min(1024, N - o)
                i = lc_i[0]
                lc_i[0] += 1
                slot = i % 2
                wsk = ("ws", slot)
                stage = ws[slot][:, 0:n]
                srcap = src[:, c * N + o:c * N + o + n]
                P.dma("sp", lambda e, a=stage, b=srcap: e.dma_start(out=a, in_=b), w=[wsk])
                dst = dst3[:, c, o:o + n]
                if scale_col is not None:
                    sc = sm[:, scale_col + c:scale_col + c + 1]
                    P.op("dve", lambda e, a=dst, b=stage, s_=sc: e.tensor_scalar(out=a, in0=b, scalar1=s_, scalar2=None, op0=ALU.mult),
                         r=[wsk, "small"], w=[(tag, c, o)])
                elif i % 4 < 2:
                    P.op("dve", lambda e, a=dst, b=stage: e.tensor_copy(out=a, in_=b), r=[wsk], w=[(tag, c, o)])
                else:
                    P.op("act", lambda e, a=dst, b=stage: e.activation(out=a, in_=b, func=AF.Copy), r=[wsk], w=[(tag, c, o)])

    def wkeys(tag, C, N):
        return [(tag, c, o) for c in range(C) for o in range(0, N, 1024)]

    def v3(ap, a, b):
        return ap.rearrange("p (a b) -> p a b", a=a, b=b)

    ABF.reset(); AFF.reset()
    WS[0] = [AFF.take(1024) for _ in range(2)]
    WA = v3(ABF.take(8 * NWA), 8, NWA)
    WV = v3(ABF.take(8 * 512), 8, 512)
    WUQ = v3(ABF.take(3 * 768), 3, 768)
    WUKV = v3(ABF.take(2 * 1024), 2, 1024)
    load_cast(WA, wa, 8, NWA, tag="WA", o_list=[1024, 0])
    load_cast(WV, wv, 8, 512, tag="WV")
    load_cast(WUQ, wuq, 3, 768, scale_col=GQ, tag="WUQ")
    load_cast(WUKV, wukv, 2, 1024, scale_col=GKV, tag="WUKV")
    kWA = wkeys("WA", 8, NWA); kWV = wkeys("WV", 8, 512); kWUQ = wkeys("WUQ", 3, 768); kWUKV = wkeys("WUKV", 2, 1024)

    xbA = [v3(ABF.take(8 * 512), 8, 512) for _ in range(2)]
    stg = [ABF.take(512) for _ in range(8)]
    sqb = [ABF.take(512) for _ in range(2)]
    cqn = v3(ABF.take(3 * 512), 3, 512)
    ckvn = v3(ABF.take(2 * 512), 2, 512)
    vmt = v3(ABF.take(4 * 512), 4, 512)
    vdt = v3(ABF.take(4 * 512), 4, 512)
    xs = [AFF.take(512) for _ in range(4)]
    posi = AFF.take(512).bitcast(I32)
    posf = AFF.take(512)
    angA = AFF.take(512)
    ang2 = AFF.take(512)
    ki = AFF.take(512).bitcast(I32)
    kf = AFF.take(512)
    rr = AFF.take(512)
    COSD, SIND, COSM, SINM = (AFF.take(512) for _ in range(4))
    cqf = v3(AFF.take(3 * 512), 3, 512)
    ckvf = v3(AFF.take(2 * 512), 2, 512)
    lnt = AFF.take(512)
    rstd = AFF.take(512)
    tmp = [AFF.take(512) for _ in range(4)]
    stg_rr = [0]
    sq_rr = [0]

    def next_bank():
        b = bank_rr[0]
        bank_rr[0] = (b + 1) % 8
        return b

    def next_stg():
        i = stg_rr[0]
        stg_rr[0] = (i + 1) % 8
        return i

    def mm_group(bank, M, lhs_list, rhs_list, rkeys, ncols=512):
        n = len(lhs_list)
        for k in range(n):
            P.op("pe", lambda e, b=bank, l=lhs_list[k], r_=rhs_list[k], st=(k == 0), sp_=(k == n - 1), M=M, nc_=ncols:
                 e.matmul(ps[b][0:M, 0:nc_], lhsT=l, rhs=r_, start=st, stop=sp_), r=rkeys, w=[("ps", bank)])

    def rope_pair(b1, b2, cosT, sinT, M, out1, out2, k1, k2, tabkeys):
        P.op("dve", lambda e: e.tensor_tensor(out=tmp[0][0:M, :], in0=ps[b1][0:M, :], in1=cosT[0:M, :], op=ALU.mult), r=[("ps", b1)] + tabkeys, w=["tmp0"])
        P.op("dve", lambda e: e.tensor_tensor(out=tmp[1][0:M, :], in0=ps[b2][0:M, :], in1=sinT[0:M, :], op=ALU.mult), r=[("ps", b2)] + tabkeys, w=["tmp1"])
        P.op("dve", lambda e: e.tensor_tensor(out=tmp[2][0:M, :], in0=ps[b2][0:M, :], in1=cosT[0:M, :], op=ALU.mult), r=[("ps", b2)] + tabkeys, w=["tmp2"])
        P.op("dve", lambda e: e.tensor_tensor(out=tmp[3][0:M, :], in0=ps[b1][0:M, :], in1=sinT[0:M, :], op=ALU.mult), r=[("ps", b1)] + tabkeys, w=["tmp3"])
        P.op("dve", lambda e: e.tensor_tensor(out=out1, in0=tmp[0][0:M, :], in1=tmp[1][0:M, :], op=ALU.subtract), r=["tmp0", "tmp1"], w=[k1])
        P.op("dve", lambda e: e.tensor_tensor(out=out2, in0=tmp[2][0:M, :], in1=tmp[3][0:M, :], op=ALU.add), r=["tmp2", "tmp3"], w=[k2])

    def sin_reduce(rrbuf, rrkey, src_ang, shift):
        a = src_ang
        if shift != 0.0:
            P.op("dve", lambda e: e.tensor_scalar(out=ang2, in0=src_ang, scalar1=float(shift), scalar2=None, op0=ALU.add), r=["angA"], w=["ang2"])
            a = ang2
        P.op("dve", lambda e, a=a: e.tensor_scalar(out=ki, in0=a, scalar1=float(1.0 / TWO_PI), scalar2=None, op0=ALU.mult), r=["angA", "ang2"], w=["ki"])
        P.op("dve", lambda e: e.tensor_copy(out=kf, in_=ki), r=["ki"], w=["kf"])
        P.op("dve", lambda e, a=a: e.scalar_tensor_tensor(out=rrbuf, in0=kf, scalar=-C1, in1=a, op0=ALU.mult, op1=ALU.add), r=["kf", "angA", "ang2"], w=rrkey)
        P.op("dve", lambda e: e.scalar_tensor_tensor(out=rrbuf, in0=kf, scalar=-C2, in1=rrbuf, op0=ALU.mult, op1=ALU.add), r=["kf"] + rrkey, w=rrkey)

    def store(dst, src, rkeys, wkeys_):
        P.dma("sp", lambda e, a=dst, b=src: e.dma_start(out=a, in_=b), r=rkeys, w=wkeys_)

    TAB = (COSD, SIND, COSM, SINM)
    TABK = [("tab", i) for i in range(4)]
    RR = (WS[0][0][:, 0:512], WS[0][0][:, 512:1024], WS[0][1][:, 0:512], WS[0][1][:, 512:1024])
    RRK = [[("rr", i), ("ws", i // 2)] for i in range(4)]

    def load_x(s, g):
        gi_ = s * NG + g
        xsl_ = gi_ % 2
        xb_ = xbA[xsl_]
        tsl_ = slice(g * 512, g * 512 + 512)
        for k in range(8):
            slot = k % 4
            P.dma("sp", lambda e, k=k, slot=slot: e.dma_start(out=xs[slot], in_=xT[s, k * 128:(k + 1) * 128, tsl_]), w=[("xs", slot)])
            P.op("dve", lambda e, k=k, slot=slot: e.tensor_copy(out=xb_[:, k, :], in_=xs[slot]), r=[("xs", slot)], w=[("xb", xsl_, k)])
        store(xbs[s, :, tsl_].rearrange("(k p) t -> p k t", p=128), xb_, [("xb", xsl_, k) for k in range(8)], [("xbs", s, g)])

    def tables_dve(s, g):
        tsl_ = slice(g * 512, g * 512 + 512)
        P.dma("sp", lambda e: e.dma_start(out=posi, in_=pos[s:s + 1, tsl_].broadcast_to([128, 512])), w=["posi"])
        P.op("dve", lambda e: e.tensor_copy(out=posf, in_=posi), r=["posi"], w=["posf"])
        for ti_, col in ((0, INVFD), (2, INVFM)):
            P.op("dve", lambda e, col=col: e.tensor_scalar(out=angA, in0=posf, scalar1=sm[:, col:col + 1], scalar2=None, op0=ALU.mult),
                 r=["posf", "small"], w=["angA"])
            sin_reduce(RR[ti_], RRK[ti_], angA, math.pi / 2.0)
            sin_reduce(RR[ti_ + 1], RRK[ti_ + 1], angA, 0.0)

    def tables_act():
        for i in range(4):
            P.op("act", lambda e, i=i: e.activation(out=TAB[i], in_=RR[i], func=AF.Sin), r=RRK[i], w=[TABK[i]])

    for s in range(NSEQ):
        for g in range(NG):
            t0 = g * 512
            tsl = slice(t0, t0 + 512)
            gidx = s * NG + g
            kCOSD, kSIND, kCOSM, kSINM = TABK
            if gidx == 0:
                tables_dve(s, g)
            xsl = gidx % 2
            xb = xbA[xsl]
            if gidx == 0:
                load_x(s, g)
            xbk = [("xb", xsl, k) for k in range(8)]
            def fm_chunk(c, M=128):
                b = next_bank()
                mm_group(b, M, [WA[:, k, c * 128:c * 128 + M] for k in range(8)], [xb[:, k, :] for k in range(8)],
                         xbk + [("WA", k, (c * 128 // 1024) * 1024) for k in range(8)])
                return b
            for (c0, nch, fbuf, nbuf, nm, dim) in ((8, 3, cqf, cqn, "cq", 384.0), (11, 2, ckvf, ckvn, "ckv", 256.0)):
                sqs = []
                for j in range(nch):
                    b = fm_chunk(c0 + j)
                    qi = sq_rr[0]; sq_rr[0] = (qi + 1) % 2
                    P.op("act", lambda e, b=b, qi=qi: e.activation(out=sqb[qi], in_=ps[b], func=AF.Square), r=[("ps", b)], w=[("sqb", qi)])
                    P.op("act", lambda e, b=b, j=j, fbuf=fbuf: e.activation(out=fbuf[:, j, :], in_=ps[b], func=AF.Copy), r=[("ps", b)], w=[(nm + "f", j)])
                    sqs.append(qi)
                    if j == 0:
                        bs = next_bank()
                    P.op("pe", lambda e, bs=bs, qi=qi, st=(j == 0), sp_=(j == nch - 1): e.matmul(ps[bs], lhsT=ones, rhs=sqb[qi], start=st, stop=sp_),
                         r=[("sqb", qi), "ones"], w=[("ps", bs)])
                P.op("act", lambda e, bs=bs, dim=dim: e.activation(out=lnt, in_=ps[bs], func=AF.Ln, scale=1.0 / dim, bias=1e-6), r=[("ps", bs)], w=["lnt"])
                P.op("act", lambda e: e.activation(out=rstd, in_=lnt, func=AF.Exp, scale=-0.5), r=["lnt"], w=["rstd"])
                for j in range(nch):
                    P.op("dve", lambda e, j=j, fbuf=fbuf, nbuf=nbuf: e.tensor_tensor(out=nbuf[:, j, :], in0=fbuf[:, j, :], in1=rstd, op=ALU.mult),
                         r=[(nm + "f", j), "rstd"], w=[(nm + "n", j)])
            cqk = [("cqn", j) for j in range(3)]
            ckvk = [("ckvn", j) for j in range(2)]
            tables_act()
            if gidx + 1 < NSEQ * NG:
                load_x((gidx + 1) // NG, (gidx + 1) % NG)
            for c in range(2, 8):
                b = fm_chunk(c)
                i = next_stg()
                P.op("act", lambda e, b=b, i=i: e.activation(out=stg[i], in_=ps[b], func=AF.Copy), r=[("ps", b)], w=[("stg", i)])
                store(fmd[s, c * 128:(c + 1) * 128, tsl], stg[i], [("stg", i)], [("fmd", s, g, c)])
            for tt in range(4):
                b = next_bank()
                mm_group(b, 128, [xb[:, k, tt * 128:(tt + 1) * 128] for k in range(8)], [WV[:, k, :] for k in range(8)], xbk + kWV)
                P.op("act", lambda e, b=b, tt=tt: e.activation(out=vdt[:, tt, :], in_=ps[b], func=AF.Copy), r=[("ps", b)], w=[("vdt", tt)])
            store(vscr[s, 0, :, g * 2048:(g + 1) * 2048], vdt.rearrange("p a b -> p (a b)"), [("vdt", tt) for tt in range(4)], [("vscr", s, 0, g)])

            b1 = fm_chunk(0)
            b2 = fm_chunk(1)
            i1, i2 = next_stg(), next_stg()
            rope_pair(b1, b2, COSD, SIND, 128, stg[i1], stg[i2], ("stg", i1), ("stg", i2), [kCOSD, kSIND])
            store(fmd[s, 0:128, tsl], stg[i1], [("stg", i1)], [("fmd", s, g, 0)])
            store(fmd[s, 128:256, tsl], stg[i2], [("stg", i2)], [("fmd", s, g, 1)])
            b = fm_chunk(13, M=48)
            i = next_stg()
            tk = [kCOSM, kSINM]
            P.op("dve", lambda e, b=b: e.tensor_tensor(out=tmp[0][0:16, :], in0=ps[b][0:16, :], in1=COSM[0:16, :], op=ALU.mult), r=[("ps", b)] + tk, w=["tmp0"])
            P.op("dve", lambda e, b=b: e.tensor_tensor(out=tmp[1][0:16, :], in0=SINM[0:16, :], in1=ps[b][32:48, :], op=ALU.mult), r=[("ps", b)] + tk, w=["tmp1"])
            P.op("dve", lambda e, b=b: e.tensor_tensor(out=tmp[2][32:48, :], in0=ps[b][32:48, :], in1=COSM[32:48, :], op=ALU.mult), r=[("ps", b)] + tk, w=["tmp2"])
            P.op("dve", lambda e, b=b: e.tensor_tensor(out=tmp[3][32:48, :], in0=SINM[32:48, :], in1=ps[b][0:16, :], op=ALU.mult), r=[("ps", b)] + tk, w=["tmp3"])
            P.op("dve", lambda e, i=i: e.tensor_tensor(out=stg[i][0:16, :], in0=tmp[0][0:16, :], in1=tmp[1][0:16, :], op=ALU.subtract), r=["tmp0", "tmp1"], w=[("stg", i)])
            P.op("dve", lambda e, i=i: e.tensor_tensor(out=stg[i][32:48, :], in0=tmp[2][32:48, :], in1=tmp[3][32:48, :], op=ALU.add), r=["tmp2", "tmp3", ("stg", i)], w=[("stg", i)])
            store(fmm[s, 1280:1296, tsl], stg[i][0:16, :], [("stg", i)], [("fmm", s, g, "kr1")])
            store(fmm[s, 1296:1312, tsl], stg[i][32:48, :], [("stg", i)], [("fmm", s, g, "kr2")])
            def q_chunk(qc):
                b = next_bank()
                mm_group(b, 128, [WUQ[:, k, qc * 128:(qc + 1) * 128] for k in range(3)], [cqn[:, k, :] for k in range(3)], cqk + kWUQ)
                return b
            b1 = q_chunk(0)
            b2 = q_chunk(1)
            i1, i2 = next_stg(), next_stg()
            rope_pair(b1, b2, COSM, SINM, 128, stg[i1], stg[i2], ("stg", i1), ("stg", i2), tk)
            store(fmm[s, 0:128, tsl], stg[i1], [("stg", i1)], [("fmm", s, g, 0)])
            store(fmm[s, 128:256, tsl], stg[i2], [("stg", i2)], [("fmm", s, g, 1)])
            for qc in range(2, 6):
                b = q_chunk(qc)
                i = next_stg()
                P.op("act", lambda e, b=b, i=i: e.activation(out=stg[i], in_=ps[b], func=AF.Copy), r=[("ps", b)], w=[("stg", i)])
                store(fmm[s, qc * 128:(qc + 1) * 128, tsl], stg[i], [("stg", i)], [("fmm", s, g, qc)])
            for kc in range(4):
                b = next_bank()
                mm_group(b, 128, [WUKV[:, k, kc * 128:(kc + 1) * 128] for k in range(2)], [ckvn[:, k, :] for k in range(2)], ckvk + kWUKV)
                i = next_stg()
                P.op("act", lambda e, b=b, i=i: e.activation(out=stg[i], in_=ps[b], func=AF.Copy), r=[("ps", b)], w=[("stg", i)])
                store(fmm[s, 768 + kc * 128:768 + (kc + 1) * 128, tsl], stg[i], [("stg", i)], [("fmm", s, g, 6 + kc)])
            for tt in range(4):
                b = next_bank()
                mm_group(b, 128, [ckvn[:, k, tt * 128:(tt + 1) * 128] for k in range(2)], [WUKV[:, k, 512:1024] for k in range(2)], ckvk + kWUKV)
                P.op("dve", lambda e, b=b, tt=tt: e.tensor_copy(out=vmt[:, tt, :], in_=ps[b]), r=[("ps", b)], w=[("vmt", tt)])
            store(vscr[s, 1, :, g * 2048:(g + 1) * 2048], vmt.rearrange("p a b -> p (a b)"), [("vmt", tt) for tt in range(4)], [("vscr", s, 1, g)])
            if gidx + 1 < NSEQ * NG:
                tables_dve((gidx + 1) // NG, (gidx + 1) % NG)
    P.barrier()

    ABF.reset(); AFF.reset()
    WS[0] = [AFF.take(1024) for _ in range(2)]
    VDs = [v3(ABF.take(32 * 128), 32, 128) for _ in range(2)]
    VM = ABF.take(32 * 8 * 128).rearrange("p (t h d) -> p t h d", t=32, h=8, d=128)
    QT = [v3(ABF.take(2 * S), 2, S) for _ in range(2)]
    KT = [ABF.take(S) for _ in range(2)]
    NPT = 4
    PTv = [AFF.take(512).bitcast(BF16).rearrange("p (a b) -> p a b", a=2, b=512) for _ in range(NPT)]
    ostg = [ABF.take(512) for _ in range(2)]
    sqeA = [ABF.take(512) for _ in range(2)]
    EA = [[AFF.take(512) for _ in range(5)] for _ in range(2)]
    Em = [AFF.take(512) for _ in range(2)]
    P.op("pool", lambda e: e.memset(VM.rearrange("p t h d -> p (t h d)"), 1.0), w=["VM"])
    sc_rr = [0]
    pt_rr = [0]
    og_rr = [0]
    acc_rr = [0]

    def load_qk(s, kind, h, slot):
        Q, K = QT[slot], KT[slot]
        if kind == "d":
            if h < 2:
                P.op("pool", lambda e, Q=Q: e.memset(Q[64:128, 0, :], 0.0), w=[("Q", slot)])
                P.op("pool", lambda e, Q=Q: e.memset(Q[0:64, 1, :], 0.0), w=[("Q", slot)])
            P.dma("sp", lambda e, h=h: e.dma_start(out=VDs[slot], in_=vscr[s, 0, :, :].rearrange("p (t c) -> p t c", t=32, c=512)[:, :, h * 128:(h + 1) * 128]),
                  w=[("VD", slot)])
            for m in range(2):
                j = 2 * h + m
                for (dst0, n, qrow, krow) in ((0, 8, j * 8, 64 + j * 8), (8, 8, 128 + j * 8, 192 + j * 8), (16, 48, 256 + j * 48, 640 + j * 48)):
                    p0 = m * 64 + dst0
                    P.dma("sp", lambda e, p0=p0, n=n, qrow=qrow, Q=Q, m=m: e.dma_start(out=Q[p0:p0 + n, m, :], in_=fmd[s, qrow:qrow + n, :]), w=[("Q", slot)])
                    P.dma("sp", lambda e, p0=p0, n=n, krow=krow, K=K: e.dma_start(out=K[p0:p0 + n, :], in_=fmd[s, krow:krow + n, :]), w=[("K", slot)])
        else:
            for (p0, n, qrow, krow) in ((0, 64, 256 + h * 64, 768 + h * 64), (64, 16, h * 16, 1280), (80, 16, 128 + h * 16, 1296)):
                P.dma("sp", lambda e, p0=p0, n=n, qrow=qrow, Q=Q: e.dma_start(out=Q[p0:p0 + n, 0, :], in_=fmm[s, qrow:qrow + n, :]), w=[("Q", slot)])
                P.dma("sp", lambda e, p0=p0, n=n, krow=krow, K=K: e.dma_start(out=K[p0:p0 + n, :], in_=fmm[s, krow:krow + n, :]), w=[("K", slot)])

    pair_rr = [0]
    ep_rr = [0]
    pending_ep = []

    def qk_pair(slot, specs, scale, npairs):
        K = KT[slot]
        pi = pair_rr[0] % npairs
        pair_rr[0] += 1
        b0 = 2 * pi
        c0s = []
        for idx, (qm, kd, kb, qg) in enumerate(specs):
            b = b0 + idx
            Q = QT[slot][:, qm, :]
            j = kb - 4 * qg
            c0 = 128 * j if j >= 0 else 0
            P.op("pe", lambda e, b=b, c0=c0, kd=kd, kb=kb, qg=qg, Q=Q, j=j: e.matmul(
                ps[b][:, c0:512], lhsT=K[0:kd, kb * 128:(kb + 1) * 128], rhs=Q[0:kd, qg * 512 + c0:(qg + 1) * 512], start=True, stop=(j < 0)),
                r=[("Q", slot), ("K", slot)], w=[("ps", b)])
            if j >= 0:
                P.op("pe", lambda e, b=b, c0=c0: e.matmul(ps[b][:, c0:c0 + 128], lhsT=ident, rhs=maskb, start=False, stop=True), r=["cbf"], w=[("ps", b)])
            c0s.append(c0)
        cu = min(c0s)
        pt = pt_rr[0]; pt_rr[0] = (pt + 1) % NPT
        P.op("act", lambda e: e.activation(out=PTv[pt][:, :, cu:512], in_=pspair(b0)[:, :, cu:512], func=AF.Exp, scale=scale),
             r=[("ps", b0), ("ps", b0 + 1)], w=[("PT", pt)])
        return pt, c0s

    def flush_ep():
        while pending_ep:
            pending_ep.pop(0)()

    def attn_diff(s, h, qg, slot):
        A = [4, 5]
        R = [6, 7]
        nkb = 4 * qg + 4
        pend = []

        def pv(kb, pt, c0s):
            last = (kb == nkb - 1)
            for m in range(2):
                c0 = c0s[m]
                P.op("pe", lambda e, m=m, c0=c0: e.matmul(ps[A[m]][:, c0:512], lhsT=VDs[slot][:, kb, :], rhs=PTv[pt][:, m, c0:512], start=(kb == 0), stop=last),
                     r=[("PT", pt), ("VD", slot)], w=[("ps", A[m])])
                P.op("pe", lambda e, m=m, c0=c0: e.matmul(ps[R[m]][:, c0:512], lhsT=ones, rhs=PTv[pt][:, m, c0:512], start=(kb == 0), stop=last),
                     r=[("PT", pt), "ones"], w=[("ps", R[m])])

        for kb in range(nkb):
            pt, c0s = qk_pair(slot, [(0, 128, kb, qg), (1, 128, kb, qg)], 0.125, 2)
            pend.append((kb, pt, c0s))
            if len(pend) > 1:
                pv(*pend.pop(0))
            if kb == 2:
                flush_ep()
        while pend:
            pv(*pend.pop(0))
        flush_ep()
        ei = ep_rr[0]; ep_rr[0] = (ei + 1) % 2
        E = EA[ei]
        sqe = sqeA[ei]
        ek = [("E", ei, i) for i in range(5)]
        P.op("act", lambda e: e.activation(out=E[2], in_=ps[A[0]], func=AF.Copy), r=[("ps", A[0])], w=[ek[2]])
        P.op("dve", lambda e: e.reciprocal(out=E[0], in_=ps[R[0]]), r=[("ps", R[0])], w=[ek[0]])
        P.op("act", lambda e: e.activation(out=E[3], in_=ps[A[1]], func=AF.Copy), r=[("ps", A[1])], w=[ek[3]])
        P.op("dve", lambda e: e.reciprocal(out=E[1], in_=ps[R[1]]), r=[("ps", R[1])], w=[ek[1]])
        P.op("dve", lambda e: e.tensor_tensor(out=E[2], in0=E[2], in1=E[0], op=ALU.mult), r=[ek[0], ek[2]], w=[ek[2]])
        P.op("dve", lambda e: e.tensor_tensor(out=E[3], in0=E[3], in1=E[1], op=ALU.mult), r=[ek[1], ek[3]], w=[ek[3]])
        P.op("dve", lambda e: e.scalar_tensor_tensor(out=E[4], in0=E[3], scalar=lw[:, NEGLAM:NEGLAM + 1], in1=E[2], op0=ALU.mult, op1=ALU.add),
             r=[ek[2], ek[3], "neglam"], w=[ek[4]])

        def part2():
            P.op("act", lambda e: e.activation(out=sqe, in_=E[4], func=AF.Square), r=[ek[4]], w=[("sqe", ei)])
            pi = pair_rr[0] % 2
            pair_rr[0] += 1
            bm = 2 * pi
            P.op("pe", lambda e: e.matmul(ps[bm], lhsT=ones, rhs=sqe, start=True, stop=True), r=[("sqe", ei), "ones"], w=[("ps", bm)])
            P.op("act", lambda e: e.activation(out=E[0], in_=ps[bm], func=AF.Ln, scale=1.0 / 128.0, bias=1e-6), r=[("ps", bm)], w=[ek[0]])
            P.op("act", lambda e: e.activation(out=E[1], in_=E[0], func=AF.Exp, scale=-0.5), r=[ek[0]], w=[ek[1]])
            og = og_rr[0]; og_rr[0] = (og + 1) % 2
            P.op("dve", lambda e: e.scalar_tensor_tensor(out=ostg[og], in0=E[4], scalar=lw[:, SUBGS:SUBGS + 1], in1=E[1], op0=ALU.mult, op1=ALU.mult),
                 r=[ek[4], ek[1], "subgs"], w=[("ostg", og)])
            store(oTs[s, h * 128:(h + 1) * 128, qg * 512:(qg + 1) * 512], ostg[og], [("ostg", og)], [("oTs", s, h, qg)])

        pending_ep.append(part2)

    def attn_mla(s, h, qg, slot):
        flush_ep()
        a = 6 + acc_rr[0]; acc_rr[0] = (acc_rr[0] + 1) % 2
        nkb = 4 * qg + 4
        pend = []
        scale = 96.0 ** -0.5

        def pv(kb0, pt, c0s):
            for idx in range(2):
                kb = kb0 + idx
                c0 = c0s[idx]
                P.op("pe", lambda e, idx=idx, kb=kb, c0=c0: e.matmul(ps[a][:, c0:512], lhsT=VM[:, kb, h, :], rhs=PTv[pt][:, idx, c0:512],
                                                                      start=(kb == 0), stop=(kb == nkb - 1)),
                     r=[("PT", pt), "VM"], w=[("ps", a)])

        for kb0 in range(0, nkb, 2):
            pt, c0s = qk_pair(slot, [(0, 96, kb0, qg), (0, 96, kb0 + 1, qg)], scale, 3)
            pend.append((kb0, pt, c0s))
            if len(pend) > 2:
                pv(*pend.pop(0))
        while pend:
            pv(*pend.pop(0))
        ei = a % 2
        P.op("dve", lambda e: e.reciprocal(out=Em[ei][64:128, :], in_=ps[a][64:128, :]), r=[("ps", a)], w=[("Em", ei)])
        og = og_rr[0]; og_rr[0] = (og + 1) % 2
        P.op("dve", lambda e: e.tensor_tensor(out=ostg[og][0:64, :], in0=ps[a][0:64, :], in1=Em[ei][64:128, :], op=ALU.mult),
             r=[("ps", a), ("Em", ei)], w=[("ostg", og)])
        store(oTs[s, 512 + h * 64:512 + (h + 1) * 64, qg * 512:(qg + 1) * 512], ostg[og][0:64, :], [("ostg", og)], [("oTs", s, 4 + h, qg)])

    jobs = []
    for s in range(NSEQ):
        for h in range(4):
            jobs.append((s, "d", h))
        for h in range(8):
            jobs.append((s, "m", h))
    for ji, (s, kind, h) in enumerate(jobs):
        slot = ji % 2
        if ji == 0:
            load_qk(s, kind, h, slot)
        if kind == "d" and h == 0:
            P.dma("sp", lambda e, s=s: e.dma_start(out=VM[:, :, :, 0:64], in_=vscr[s, 1, :, :].rearrange("p (t h d) -> p t h d", t=32, h=8, d=64)), w=["VM"])
        if ji + 1 < len(jobs):
            load_qk(*jobs[ji + 1], (ji + 1) % 2)
        for qg in range(NG):
            if kind == "d":
                attn_diff(s, h, qg, slot)
            else:
                attn_mla(s, h, qg, slot)
    flush_ep()

    P.barrier()

    ABF.reset(); AFF.reset()
    WS[0] = [AFF.take(1024) for _ in range(2)]
    WG = v3(ABF.take(8 * 2048), 8, 2048)
    WOD = v3(ABF.take(4 * 1024), 4, 1024)
    WOM = v3(ABF.take(4 * 1024), 4, 1024)
    WOUT = v3(ABF.take(8 * 1024), 8, 1024)
    load_cast(WG, wg, 8, 2048, tag="WG")
    load_cast(WOD, wod, 4, 1024, tag="WOD")
    load_cast(WOM, wom, 4, 1024, tag="WOM")
    load_cast(WOUT, wout, 8, 1024, tag="WOUT")
    kWG = wkeys("WG", 8, 2048); kWOD = wkeys("WOD", 4, 1024); kWOM = wkeys("WOM", 4, 1024); kWOUT = wkeys("WOUT", 8, 1024)
    xb2A = [v3(ABF.take(8 * 512), 8, 512) for _ in range(2)]
    oTgA = [v3(ABF.take(8 * 512), 8, 512) for _ in range(2)]
    yTA = [v3(ABF.take(8 * 512), 8, 512) for _ in range(2)]
    x1bA = [ABF.take(1024) for _ in range(2)]
    x1TA = [v3(ABF.take(8 * 128), 8, 128) for _ in range(2)]
    xtokA = [AFF.take(1024) for _ in range(5)]
    xt_rr = [0]
    G0, G1, TT_, UU_ = (AFF.take(512) for _ in range(4))
    rbufA = [AFF.take(1024) for _ in range(2)]
    x1fA = [AFF.take(1024) for _ in range(2)]
    lng = AFF.take(1024)
    lnb = AFF.take(1024)
    st6A = [stt[:, 0, :, :], stt[:, 1, :, :]]
    mvA = [mvt[:, 0, :], mvt[:, 1, :]]
    ln_rr = [0]

    dq = []

    def dq_pop(n=1):
        for _ in range(n):
            if dq:
                dq.pop(0)()

    def dq_flush():
        while dq:
            dq.pop(0)()

    def ln_stats(rin, rkey):
        li = ln_rr[0]; ln_rr[0] = (li + 1) % 2
        st6 = st6A[li]; mv = mvA[li]
        P.op("dve", lambda e: e.bn_stats(out=st6[:, 0, :], in_=rin[:, 0:512]), r=[rkey], w=[("st0", li)])
        P.op("dve", lambda e: e.bn_stats(out=st6[:, 1, :], in_=rin[:, 512:1024]), r=[rkey], w=[("st1", li)])
        P.op("dve", lambda e: e.bn_aggr(out=mv[:, 0:2], in_=st6), r=[("st0", li), ("st1", li)], w=[("mv", li)])
        P.op("act", lambda e: e.activation(out=mv[:, 2:3], in_=mv[:, 1:2], func=AF.Ln, bias=1e-5), r=[("mv", li)], w=[("mv2", li)])
        P.op("act", lambda e: e.activation(out=mv[:, 3:4], in_=mv[:, 2:3], func=AF.Exp, scale=-0.5), r=[("mv2", li)], w=[("mv3", li)])
        return li

    def layer_norm(rin, rkey, gam, bet, gkeys, out, okey, xnb=None, xnbkey=None, tail=None, defer_all=False):
        def stats_and_norm():
            li = ln_stats(rin, rkey)
            mv = mvA[li]
            if xnb is not None:
                P.op("dve", lambda e: e.tensor_scalar(out=xnb, in0=rin, scalar1=mv[:, 0:1], scalar2=mv[:, 3:4], op0=ALU.subtract, op1=ALU.mult),
                     r=[rkey, ("mv", li), ("mv3", li)], w=[xnbkey])
            P.op("dve", lambda e: e.tensor_scalar(out=rin, in0=rin, scalar1=mv[:, 0:1], scalar2=mv[:, 3:4], op0=ALU.subtract, op1=ALU.mult),
                 r=[rkey, ("mv", li), ("mv3", li)], w=[rkey])

        def g_op():
            P.op("dve", lambda e: e.tensor_tensor(out=rin, in0=rin, in1=gam, op=ALU.mult), r=[rkey] + gkeys, w=[rkey])

        def b_op():
            P.op("dve", lambda e: e.tensor_tensor(out=out, in0=rin, in1=bet, op=ALU.add), r=[rkey] + gkeys, w=[okey])
            if tail is not None:
                tail()

        if defer_all:
            dq.append(stats_and_norm)
        else:
            stats_and_norm()
        dq.append(g_op)
        dq.append(b_op)

    P.dma("sp", lambda e: e.dma_start(out=lng, in_=lnp[0:1, :].broadcast_to([128, 1024])), w=["lng"])
    P.dma("sp", lambda e: e.dma_start(out=lnb, in_=lnp[1:2, :].broadcast_to([128, 1024])), w=["lnb"])

    def c_loads(s, g):
        t0 = g * 512
        tsl = slice(t0, t0 + 512)
        gsl = (s * NG + g) % 2
        ctx = dict(s=s, g=g, t0=t0, gsl=gsl, xb2=xb2A[gsl], oTg=oTgA[gsl], yT=yTA[gsl], kxb2=("xb2", gsl), koTg=("oTg", gsl), xts=[])
        P.dma("sp", lambda e, xb2=ctx["xb2"]: e.dma_start(out=xb2, in_=xbs[s, :, tsl].rearrange("(k p) t -> p k t", p=128)), w=[ctx["kxb2"]])
        P.dma("sp", lambda e, oTg=ctx["oTg"]: e.dma_start(out=oTg, in_=oTs[s, :, tsl].rearrange("(k p) t -> p k t", p=128)), w=[ctx["koTg"]])
        return ctx

    def c_fchunk(ctx, f):
        xb2, oTg, yT, kxb2, koTg, gsl = ctx["xb2"], ctx["oTg"], ctx["yT"], ctx["kxb2"], ctx["koTg"], ctx["gsl"]
        ba = next_bank()
        mm_group(ba, 128, [WG[:, k, f * 128:(f + 1) * 128] for k in range(8)], [xb2[:, k, :] for k in range(8)], [kxb2] + kWG)
        bb = next_bank()
        mm_group(bb, 128, [WG[:, k, 1024 + f * 128:1024 + (f + 1) * 128] for k in range(8)], [xb2[:, k, :] for k in range(8)], [kxb2] + kWG)
        bc = next_bank()
        mm_group(bc, 128, [WOD[:, k, f * 128:(f + 1) * 128] for k in range(4)], [oTg[:, k, :] for k in range(4)], [koTg] + kWOD)
        bd = next_bank()
        mm_group(bd, 128, [WOM[:, k, f * 128:(f + 1) * 128] for k in range(4)], [oTg[:, 4 + k, :] for k in range(4)], [koTg] + kWOM)
        P.op("act", lambda e: e.activation(out=G0, in_=ps[ba], func=AF.Sigmoid, bias=sm[:, GATEB + f:GATEB + f + 1]), r=[("ps", ba), "small"], w=["G0"])
        P.op("act", lambda e: e.activation(out=G1, in_=ps[bb], func=AF.Sigmoid, bias=sm[:, GATEB + 8 + f:GATEB + 9 + f]), r=[("ps", bb), "small"], w=["G1"])
        P.op("dve", lambda e: e.tensor_tensor(out=TT_, in0=ps[bc], in1=G0, op=ALU.mult), r=[("ps", bc), "G0"], w=["TT"])
        P.op("dve", lambda e: e.tensor_tensor(out=UU_, in0=ps[bd], in1=G1, op=ALU.mult), r=[("ps", bd), "G1"], w=["UU"])
        P.op("dve", lambda e: e.tensor_tensor(out=yT[:, f, :], in0=TT_, in1=UU_, op=ALU.add), r=["TT", "UU"], w=[("yT", gsl, f)])

    tile_rr = [0]

    def c_tile_a(ctx, tt):
        s, t0, yT, gsl = ctx["s"], ctx["t0"], ctx["yT"], ctx["gsl"]
        ti = tile_rr[0]; tile_rr[0] = (ti + 1) % 2
        xi = xt_rr[0]; xt_rr[0] = (xi + 1) % 5
        P.dma("sp", lambda e: e.dma_start(out=xtokA[xi], in_=x[s, t0 + tt * 128:t0 + (tt + 1) * 128, :]), w=[("xtok", xi)])
        rbuf, x1f, x1b = rbufA[ti], x1fA[ti], x1bA[ti]
        yk = [("yT", gsl, f) for f in range(8)]
        for n in range(2):
            b = next_bank()
            mm_group(b, 128, [yT[:, k, tt * 128:(tt + 1) * 128] for k in range(8)], [WOUT[:, k, n * 512:(n + 1) * 512] for k in range(8)], yk + kWOUT)
            P.op("dve", lambda e, b=b, n=n: e.scalar_tensor_tensor(out=rbuf[:, n * 512:(n + 1) * 512], in0=xtokA[xi][:, n * 512:(n + 1) * 512],
                                                                  scalar=ALPHA, in1=ps[b], op0=ALU.mult, op1=ALU.add),
                 r=[("ps", b), ("xtok", xi)], w=[("rbuf", ti)])
        row0 = s * S + t0 + tt * 128
        layer_norm(rbuf, ("rbuf", ti), lng, lnb, ["lng", "lnb"], x1f, ("x1f", ti), xnb=x1b, xnbkey=("x1b", ti),
                   tail=lambda: store(x1s[row0:row0 + 128, :], x1f, [("x1f", ti)], [("x1s", row0)]))
        return ti

    def c_tile_b(ctx, tt, ti):
        s, t0 = ctx["s"], ctx["t0"]
        x1b, x1T = x1bA[ti], x1TA[ti]
        b = next_bank()
        psb = ps[b].bitcast(BF16)
        for k in range(8):
            P.op("pe", lambda e, k=k: e.transpose(psb[:, k * 128:(k + 1) * 128], x1b[:, k * 128:(k + 1) * 128], ident),
                 r=[("x1b", ti), "cbf"], w=[("ps", b)])
        for k in range(8):
            P.op("act", lambda e, k=k: e.activation(out=x1T[:, k, :], in_=psb[:, k * 128:(k + 1) * 128], func=AF.Identity,
                                                    scale=sm[:, LN1G + k:LN1G + k + 1], bias=sm[:, LN1B + k:LN1B + k + 1]),
                 r=[("ps", b), "small"], w=[("x1T", ti)])
        c0 = s * S + t0 + tt * 128
        P.dma("sp", lambda e: e.dma_start(out=x1Ts[:, c0:c0 + 128].rearrange("(k p) t -> p k t", p=128), in_=x1T), r=[("x1T", ti)], w=[("x1Ts", c0)])

    cgroups = [(s, g) for s in range(NSEQ) for g in range(NG)]
    prev = None
    for gi in range(len(cgroups) + 1):
        cur = c_loads(*cgroups[gi]) if gi < len(cgroups) else None
        for i in range(4):
            ti = None
            if prev is not None:
                ti = c_tile_a(prev, i)
            if cur is not None:
                c_fchunk(cur, 2 * i)
                dq_pop()
                c_fchunk(cur, 2 * i + 1)
                dq_pop()
            else:
                dq_flush()
            if prev is not None:
                c_tile_b(prev, i, ti)
        prev = cur
    dq_flush()

    P.barrier()

    ABF.reset(); AFF.reset()
    WS[0] = [AFF.take(1024) for _ in range(2)]
    WUP = v3(ABF.take(8 * 4096), 8, 4096)
    WDN = v3(ABF.take(32 * 1024), 32, 1024)
    for j4 in range(4):
        load_cast(WUP, wup, 8, 4096, tag="WUP", o_list=[j4 * 1024])
        load_cast(WDN, wdn, 32, 1024, tag="WDN", c_list=range(8 * j4, 8 * j4 + 8))
    kWUP = wkeys("WUP", 8, 4096); kWDN = wkeys("WDN", 32, 1024)
    x1TgA = [v3(ABF.take(8 * 256), 8, 256) for _ in range(2)]
    hT = [AFF.take(128).bitcast(BF16) for _ in range(3)]
    x1gA = [v3(AFF.take(2 * 1024), 2, 1024) for _ in range(2)]
    sqv = [AFF.take(256) for _ in range(2)]
    rb2A = [AFF.take(1024) for _ in range(2)]
    ob = [AFF.take(1024) for _ in range(2)]
    lng2 = AFF.take(1024)
    lnb2 = AFF.take(1024)
    P.dma("sp", lambda e: e.dma_start(out=lng2, in_=lnp[2:3, :].broadcast_to([128, 1024])), w=["lng2"])
    P.dma("sp", lambda e: e.dma_start(out=lnb2, in_=lnp[3:4, :].broadcast_to([128, 1024])), w=["lnb2"])
    ub_rr = [0]
    for gi in range(NSEQ * S // 256):
        r0 = gi * 256
        x1Tg = x1TgA[gi % 2]
        x1g = x1gA[gi % 2]
        kx1T = ("x1Tg", gi % 2)
        kx1g = ("x1g", gi % 2)
        P.dma("sp", lambda e, r0=r0, x1Tg=x1Tg: e.dma_start(out=x1Tg, in_=x1Ts[:, r0:r0 + 256].rearrange("(k p) t -> p k t", p=128)), w=[kx1T])
        P.dma("sp", lambda e, r0=r0, x1g=x1g: e.dma_start(out=x1g, in_=x1s[r0:r0 + 256, :].rearrange("(a p) d -> p a d", p=128)), w=[kx1g])
        pend = []

        def down(fc, hi):
            for tt in range(2):
                for n in range(2):
                    a = 4 + tt * 2 + n
                    P.op("pe", lambda e, a=a, tt=tt, n=n, fc=fc, hi=hi: e.matmul(ps[a], lhsT=hT[hi][:, tt * 128:(tt + 1) * 128], rhs=WDN[:, fc, n * 512:(n + 1) * 512],
                                                                               start=(fc == 0), stop=(fc == 31)),
                         r=[("hT", hi), ("WDN", fc, 0)], w=[("ps", a)])

        for fc in range(32):
            ub = ub_rr[0]; ub_rr[0] = (ub + 1) % 4
            mm_group(ub, 128, [WUP[:, k, fc * 128:(fc + 1) * 128] for k in range(8)], [x1Tg[:, k, :] for k in range(8)],
                     [kx1T] + [("WUP", k, (fc * 128 // 1024) * 1024) for k in range(8)], ncols=256)
            qi = fc % 2
            hi = fc % 3
            P.op("act", lambda e, ub=ub, qi=qi: e.activation(out=sqv[qi], in_=ps[ub][:, 0:256], func=AF.Square), r=[("ps", ub)], w=[("sqv", qi)])
            P.op("dve", lambda e, ub=ub, qi=qi, hi=hi: e.scalar_tensor_tensor(out=hT[hi], in0=ps[ub][:, 0:256], scalar=0.0, in1=sqv[qi], op0=ALU.is_gt, op1=ALU.mult),
                 r=[("ps", ub), ("sqv", qi)], w=[("hT", hi)])
            pend.append((fc, hi))
            if len(pend) > 1:
                down(*pend.pop(0))
            if fc % 3 == 2:
                dq_pop()
        while pend:
            down(*pend.pop(0))
        dq_flush()
        for tt in range(2):
            for n in range(2):
                a = 4 + tt * 2 + n
                P.op("dve", lambda e, a=a, tt=tt, n=n, x1g=x1g: e.scalar_tensor_tensor(out=rb2A[tt][:, n * 512:(n + 1) * 512], in0=x1g[:, tt, n * 512:(n + 1) * 512],
                                                                             scalar=ALPHA, in1=ps[a], op0=ALU.mult, op1=ALU.add),
                     r=[("ps", a), kx1g], w=[("rb2", tt)])
        for tt in range(2):
            layer_norm(rb2A[tt], ("rb2", tt), lng2, lnb2, ["lng2", "lnb2"], ob[tt], ("ob", tt), defer_all=True,
                       tail=lambda tt=tt, r0=r0: store(y[r0 + tt * 128:r0 + (tt + 1) * 128, :], ob[tt], [("ob", tt)], [("y", r0, tt)]))
    dq_flush()

    P.barrier()

    with nc.Block() as block:
        semh = {}
        import contextlib
        with contextlib.ExitStack() as es:
            for e_ in ("pe", "act", "dve", "pool"):
                semh[e_] = es.enter_context(nc.semaphore("c_" + e_))
            for i in range(NDMA):
                semh[("dma", i)] = es.enter_context(nc.semaphore(f"d{i}"))

            @block.sync
            def _(e):
                P.emit("sp", e, semh)

            @block.tensor
            def _(e):
                P.emit("pe", e, semh)

            @block.scalar
            def _(e):
                P.emit("act", e, semh)

            @block.vector
            def _(e):
                P.emit("dve", e, semh)

            @block.gpsimd
            def _(e):
                P.emit("pool", e, semh)
    nc._prog_marks = P.marks
    return nc


def _chunked(w, C):
    K, N = w.shape
    assert K == C * 128
    return np.ascontiguousarray(w.reshape(C, 128, N).transpose(1, 0, 2).reshape(128, C * N))


_NC_CACHE = {}


def kernel(x, positions, w_in, gate_b, diff_lambda, diff_subln_g, mla_q_norm_g, w_uq,
           mla_kv_norm_g, w_ukv, w_o_diff, w_o_mla, w_out, ln1_g, ln1_b, w_up, w_down,
           ln2_g, ln2_b):
    f32 = np.float32
    x = np.asarray(x, f32)
    positions = np.asarray(positions, np.int32)
    w_in0 = np.asarray(w_in, f32)[0]
    j = np.arange(8)
    i8 = np.arange(8)
    dq_x1 = (64 * j[:, None] + i8[None, :]).reshape(-1)
    dq_x2 = dq_x1 + 8
    dq_np = (64 * j[:, None] + 16 + np.arange(48)[None, :]).reshape(-1)
    cols = np.concatenate([
        dq_x1, 512 + dq_x1,
        dq_x2, 512 + dq_x2,
        dq_np, 512 + dq_np,
        np.arange(1536, 1920),
        np.arange(1920, 2176),
        np.arange(2176, 2192), np.arange(2176, 2192), np.arange(2192, 2208),
    ])
    assert cols.size == NWA
    wa = _chunked(w_in0[:, cols], 8)
    wv = _chunked(w_in0[:, 1024:1536], 8)
    wg = _chunked(w_in0[:, 2208:4256], 8)
    h8 = np.arange(8)
    i16 = np.arange(16)
    uq_cols = np.concatenate([
        (96 * h8[:, None] + 64 + i16[None, :]).reshape(-1),
        (96 * h8[:, None] + 80 + i16[None, :]).reshape(-1),
        (96 * h8[:, None] + np.arange(64)[None, :]).reshape(-1),
    ])
    wuq = _chunked(np.asarray(w_uq, f32)[0][:, uq_cols], 3)
    ukv_cols = np.concatenate([
        (128 * h8[:, None] + np.arange(64)[None, :]).reshape(-1),
        (128 * h8[:, None] + 64 + np.arange(64)[None, :]).reshape(-1),
    ])
    wukv = _chunked(np.asarray(w_ukv, f32)[0][:, ukv_cols], 2)
    wod = _chunked(np.asarray(w_o_diff, f32)[0], 4)
    wom = _chunked(np.asarray(w_o_mla, f32)[0], 4)
    woutc = _chunked(np.asarray(w_out, f32)[0], 8)
    wupc = _chunked(np.asarray(w_up, f32)[0], 8)
    wdnc = _chunked(np.asarray(w_down, f32)[0], 32)
    small = np.zeros((128, 48), f32)
    small[:, 0:3] = np.asarray(mla_q_norm_g, f32)[0].reshape(3, 128).T
    small[:, 3:5] = np.asarray(mla_kv_norm_g, f32)[0].reshape(2, 128).T
    small[:, 5] = np.asarray(diff_subln_g, f32)[0]
    p = np.arange(128)
    small[:, 6] = np.power(np.float32(500000.0), -(p % 8).astype(f32) / np.float32(8)).astype(f32)
    small[:, 7] = np.power(np.float32(500000.0), -(p % 16).astype(f32) / np.float32(16)).astype(f32)
    gb = np.asarray(gate_b, f32)[0]
    small[:, 8:24] = gb.reshape(2, 8, 128).transpose(2, 0, 1).reshape(128, 16)
    small[:, 24:32] = np.asarray(ln1_g, f32)[0].reshape(8, 128).T
    small[:, 32:40] = np.asarray(ln1_b, f32)[0].reshape(8, 128).T
    lam = np.asarray(diff_lambda, f32)[0].reshape(1, 256)
    lnp = np.stack([np.asarray(a, f32)[0] for a in (ln1_g, ln1_b, ln2_g, ln2_b)], 0)
    cb = np.zeros((128, 256), f32)
    kk = np.arange(128)
    cb[:, 0:128] = np.where(kk[None, :] >= kk[:, None], 0.0, -30000.0)
    cb[:, 128:256] = np.eye(128)
    cbf = cb.astype(ml_dtypes.bfloat16)

    if "nc" not in _NC_CACHE:
        _NC_CACHE["nc"] = build()
    nc = _NC_CACHE["nc"]

    shared = dict(wa=wa, wv=wv, wg=wg, wuq=wuq, wukv=wukv, wod=wod, wom=wom, wout=woutc, wup=wupc, wdn=wdnc,
                  small=small, lam=lam, lnp=np.ascontiguousarray(lnp), cbf=cbf)
    in_maps = []
    for c in range(8):
        xc = np.ascontiguousarray(x[2 * c:2 * c + 2])
        m = dict(shared)
        m["x"] = xc
        m["xT"] = np.ascontiguousarray(xc.transpose(0, 2, 1))
        m["pos"] = np.ascontiguousarray(positions[2 * c:2 * c + 2])
        in_maps.append(m)
    res = run_bass_kernel_spmd(nc, in_maps, core_ids=list(range(8)))
    outs = [np.asarray(r["y"], f32).reshape(NSEQ, S, D) for r in res.results]
    return np.concatenate(outs, axis=0)
```

```python
import math
import numpy as np
import ml_dtypes
import concourse.bass as bass
import concourse.mybir as mybir
from concourse.bass_utils import run_bass_kernel_spmd

F32 = mybir.dt.float32
BF16 = mybir.dt.bfloat16
I32 = mybir.dt.int32
AF = mybir.ActivationFunctionType
ALU = mybir.AluOpType

S = 4096
D = 1024
NSEQ = 2
NG = 8
ALPHA = 2.0 ** 0.25
LAMBDA_INIT = 0.8 - 0.6 * math.exp(0.0)
NWA = 1712
NDMA = 32
TWO_PI = 2.0 * math.pi
C1 = 6.28125
C2 = TWO_PI - C1
PI_LO = 3.1415925


class Prog:
    ENGS = ("pe", "act", "dve", "pool", "sp")

    def __init__(self):
        self.ops = {e: [] for e in self.ENGS}
        self.count = {e: 0 for e in self.ENGS}
        self.waited = {e: {} for e in self.ENGS}
        self.res_w = {}
        self.res_r = {}
        self.dma_val = [0] * NDMA
        self.dma_rr = 0
        self.marks = []

    def _deps(self, eng, r, w):
        need = {}

        def add(tok):
            if tok is None:
                return
            sk, v = tok
            if sk == "pe" and eng == "pe":
                return
            if v > need.get(sk, 0):
                need[sk] = v

        for k in r:
            add(self.res_w.get(k))
        for k in w:
            add(self.res_w.get(k))
            for sk, v in self.res_r.get(k, {}).items():
                add((sk, v))
        waits = []
        wd = self.waited[eng]
        for sk, v in need.items():
            if v > wd.get(sk, 0):
                wd[sk] = v
                waits.append((sk, v))
        return waits

    def _commit(self, tok, r, w):
        sk, v = tok
        for k in r:
            d = self.res_r.setdefault(k, {})
            if v > d.get(sk, 0):
                d[sk] = v
        for k in w:
            self.res_w[k] = tok
            self.res_r[k] = {}

    def op(self, eng, fn, r=(), w=()):
        waits = self._deps(eng, r, w)
        self.count[eng] += 1
        tok = (eng, self.count[eng])
        self._commit(tok, r, w)
        self.ops[eng].append((waits, fn, (eng, 1)))

    def dma(self, q, fn, r=(), w=()):
        waits = self._deps(q, r, w)
        i = self.dma_rr
        self.dma_rr = (i + 1) % NDMA
        sk = ("dma", i)
        prev = self.dma_val[i]
        wd = self.waited[q]
        if prev > wd.get(sk, 0):
            wd[sk] = prev
            waits.append((sk, prev))
        self.dma_val[i] = prev + 16
        tok = (sk, prev + 16)
        self._commit(tok, r, w)
        self.ops[q].append((waits, fn, (sk, 16)))

    def barrier(self):
        self.marks.append(dict(self.count))
        for e in self.ENGS:
            waits = []
            wd = self.waited[e]
            for o in ("pe", "act", "dve", "pool"):
                v = self.count[o]
                if o != e and v > wd.get(o, 0):
                    wd[o] = v
                    waits.append((o, v))
                if o == e and o != "pe" and v > wd.get(o, 0):
                    wd[o] = v
                    waits.append((o, v))
            for i in range(NDMA):
                sk = ("dma", i)
                v = self.dma_val[i]
                if v > wd.get(sk, 0):
                    wd[sk] = v
                    waits.append((sk, v))
            self.ops[e].append((waits, None, None))
        self.res_w = {}
        self.res_r = {}

    def emit(self, eng, e, sems):
        for waits, fn, inc in self.ops[eng]:
            for sk, v in waits:
                e.wait_ge(sems[sk], v)
            if fn is not None:
                ins = fn(e)
                ins.then_inc(sems[inc[0]], inc[1])


class Arena:
    def __init__(self, ap, n):
        self.ap = ap
        self.n = n
        self.off = 0

    def reset(self):
        self.off = 0

    def take(self, n):
        o = self.off
        self.off += n
        assert self.off <= self.n, (self.off, self.n)
        return self.ap[:, o:o + n]


def build():
    nc = bass.Bass("TRN2", target_bir_lowering=False)
    P = Prog()

    def din(name, shape, dt=F32):
        return nc.dram_tensor(name, shape, dt, kind="ExternalInput").ap()

    def dscr(name, shape, dt):
        return nc.dram_tensor(name, shape, dt, kind="Internal").ap()

    xT = din("xT", [NSEQ, D, S])
    x = din("x", [NSEQ, S, D])
    pos = din("pos", [NSEQ, S], I32)
    wa = din("wa", [128, 8 * NWA])
    wv = din("wv", [128, 8 * 512])
    wg = din("wg", [128, 8 * 2048])
    wuq = din("wuq", [128, 3 * 768])
    wukv = din("wukv", [128, 2 * 1024])
    wod = din("wod", [128, 4 * 1024])
    wom = din("wom", [128, 4 * 1024])
    wout = din("wout", [128, 8 * 1024])
    wup = din("wup", [128, 8 * 4096])
    wdn = din("wdn", [128, 32 * 1024])
    small = din("small", [128, 48])
    lam = din("lam", [1, 256])
    lnp = din("lnp", [4, 1024])
    cbf = din("cbf", [128, 256], BF16)
    y = nc.dram_tensor("y", [NSEQ * S, D], F32, kind="ExternalOutput").ap()

    fmd = dscr("fmd", [NSEQ, 1024, S], BF16)
    fmm = dscr("fmm", [NSEQ, 1312, S], BF16)
    vscr = dscr("vscr", [NSEQ, 2, 128, 32 * 512], BF16)
    xbs = dscr("xbs", [NSEQ, D, S], BF16)
    oTs = dscr("oTs", [NSEQ, D, S], BF16)
    x1s = dscr("x1s", [NSEQ * S, D], F32)
    x1Ts = dscr("x1Ts", [D, NSEQ * S], BF16)

    NBF = 69632
    NF = 15872
    abf_t = nc.alloc_sbuf_tensor("abf", [128, NBF], BF16)
    af_t = nc.alloc_sbuf_tensor("af", [128, NF], F32)
    smallt = nc.alloc_sbuf_tensor("smallt", [128, 48], F32)
    lamt = nc.alloc_sbuf_tensor("lamt", [128, 256], F32)
    lamw = nc.alloc_sbuf_tensor("lamw", [128, 8], F32)
    cbft = nc.alloc_sbuf_tensor("cbft", [128, 256], BF16)
    onest = nc.alloc_sbuf_tensor("onest", [128, 128], BF16)
    stt = nc.alloc_sbuf_tensor("stt", [128, 2, 2, 6], F32)
    mvt = nc.alloc_sbuf_tensor("mvt", [128, 2, 4], F32)
    psall_t = nc.alloc_psum_tensor("psall", [128, 4096], F32)
    psall = psall_t[:]
    ps = [psall[:, i * 512:(i + 1) * 512] for i in range(8)]

    def pspair(b0):
        return psall[:, b0 * 512:(b0 + 2) * 512].rearrange("p (a b) -> p a b", a=2, b=512)
    ABF = Arena(abf_t[:], NBF)
    AFF = Arena(af_t[:], NF)
    sm = smallt[:]
    maskb = cbft[:, 0:128]
    ident = cbft[:, 128:256]
    ones = onest[:]

    GQ, GKV, SUBG, INVFD, INVFM, GATEB, LN1G, LN1B = 0, 3, 5, 6, 7, 8, 24, 32
    NEGLAM, SUBGS = 0, 1

    P.dma("sp", lambda e: e.dma_start(out=sm, in_=small[:, :]), w=["small"])
    P.dma("sp", lambda e: e.dma_start(out=lamt[:], in_=lam[0:1, :].broadcast_to([128, 256])), w=["lamt"])
    P.dma("sp", lambda e: e.dma_start(out=cbft[:], in_=cbf[:, :]), w=["cbf"])
    P.op("pool", lambda e: e.memset(ones, 1.0), w=["ones"])
    lw = lamw[:]
    P.op("dve", lambda e: e.tensor_tensor(out=lamt[:, 0:64], in0=lamt[:, 0:64], in1=lamt[:, 64:128], op=ALU.mult), r=["lamt"], w=["lamA"])
    P.op("dve", lambda e: e.tensor_tensor(out=lamt[:, 128:192], in0=lamt[:, 128:192], in1=lamt[:, 192:256], op=ALU.mult), r=["lamt"], w=["lamB"])
    P.op("dve", lambda e: e.reduce_sum(out=lw[:, 2:3], in_=lamt[:, 0:64], axis=mybir.AxisListType.X), r=["lamA"], w=["lw2"])
    P.op("dve", lambda e: e.reduce_sum(out=lw[:, 3:4], in_=lamt[:, 128:192], axis=mybir.AxisListType.X), r=["lamB"], w=["lw3"])
    P.op("act", lambda e: e.activation(out=lw[:, 4:6], in_=lw[:, 2:4], func=AF.Exp), r=["lw2", "lw3"], w=["lw45"])
    P.op("dve", lambda e: e.tensor_tensor(out=lw[:, 6:7], in0=lw[:, 5:6], in1=lw[:, 4:5], op=ALU.subtract), r=["lw45"], w=["lw6"])
    P.op("dve", lambda e: e.tensor_scalar(out=lw[:, NEGLAM:NEGLAM + 1], in0=lw[:, 6:7], scalar1=-LAMBDA_INIT, scalar2=None, op0=ALU.add), r=["lw6"], w=["neglam"])
    P.op("dve", lambda e: e.tensor_scalar(out=lw[:, SUBGS:SUBGS + 1], in0=sm[:, SUBG:SUBG + 1], scalar1=1.0 - LAMBDA_INIT, scalar2=None, op0=ALU.mult), r=["small"], w=["subgs"])

    bank_rr = [0]
    WS = [None]

    lc_i = [0]

    def load_cast(dst3, src, C, N, scale_col=None, tag="w", c_list=None, o_list=None):
        ws = WS[0]
        c_list = list(range(C)) if c_list is None else list(c_list)
        o_list = list(range(0, N, 1024)) if o_list is None else list(o_list)
        for o in o_list:
            for c in c_list:
                n = min(1024, N - o)
                i = lc_i[0]
                lc_i[0] += 1
                slot = i % 2
                wsk = ("ws", slot)
                stage = ws[slot][:, 0:n]
                srcap = src[:, c * N + o:c * N + o + n]
                P.dma("sp", lambda e, a=stage, b=srcap: e.dma_start(out=a, in_=b), w=[wsk])
                dst = dst3[:, c, o:o + n]
                if scale_col is not None:
                    sc = sm[:, scale_col + c:scale_col + c + 1]
                    P.op("dve", lambda e, a=dst, b=stage, s_=sc: e.tensor_scalar(out=a, in0=b, scalar1=s_, scalar2=None, op0=ALU.mult),
                         r=[wsk, "small"], w=[(tag, c, o)])
                elif i % 4 < 2:
                    P.op("dve", lambda e, a=dst, b=stage: e.tensor_copy(out=a, in_=b), r=[wsk], w=[(tag, c, o)])
                else:
                    P.op("act", lambda e, a=dst, b=stage: e.activation(out=a, in_=b, func=AF.Copy), r=[wsk], w=[(tag, c, o)])

    def wkeys(tag, C, N):
        return [(tag, c, o) for c in range(C) for o in range(0, N, 1024)]

    def v3(ap, a, b):
        return ap.rearrange("p (a b) -> p a b", a=a, b=b)

    ABF.reset(); AFF.reset()
    WS[0] = [AFF.take(1024) for _ in range(2)]
    WA = v3(ABF.take(8 * NWA), 8, NWA)
    WV = v3(ABF.take(8 * 512), 8, 512)
    WUQ = v3(ABF.take(3 * 768), 3, 768)
    WUKV = v3(ABF.take(2 * 1024), 2, 1024)
    kWA = wkeys("WA", 8, NWA); kWV = wkeys("WV", 8, 512); kWUQ = wkeys("WUQ", 3, 768); kWUKV = wkeys("WUKV", 2, 1024)

    xbA = [v3(ABF.take(8 * 512), 8, 512) for _ in range(2)]
    stg = [ABF.take(512) for _ in range(8)]
    sqb = [ABF.take(512) for _ in range(2)]
    cqn = v3(ABF.take(3 * 512), 3, 512)
    ckvn = v3(ABF.take(2 * 512), 2, 512)
    vmt = v3(ABF.take(4 * 512), 4, 512)
    vdt = v3(ABF.take(4 * 512), 4, 512)
    xs = [AFF.take(512) for _ in range(4)]
    posi = AFF.take(512).bitcast(I32)
    posf = AFF.take(512)
    angA = AFF.take(512)
    ang2 = AFF.take(512)
    ki = AFF.take(512).bitcast(I32)
    kf = AFF.take(512)
    rr = AFF.take(512)
    COSD, SIND, COSM, SINM = (AFF.take(512) for _ in range(4))
    cqf = v3(AFF.take(3 * 512), 3, 512)
    ckvf = v3(AFF.take(2 * 512), 2, 512)
    lnt = AFF.take(512)
    rstd = AFF.take(512)
    tmp = [AFF.take(512) for _ in range(4)]
    stg_rr = [0]
    sq_rr = [0]

    def next_bank():
        b = bank_rr[0]
        bank_rr[0] = (b + 1) % 8
        return b

    def next_stg():
        i = stg_rr[0]
        stg_rr[0] = (i + 1) % 8
        return i

    def mm_group(bank, M, lhs_list, rhs_list, rkeys, ncols=512):
        n = len(lhs_list)
        for k in range(n):
            P.op("pe", lambda e, b=bank, l=lhs_list[k], r_=rhs_list[k], st=(k == 0), sp_=(k == n - 1), M=M, nc_=ncols:
                 e.matmul(ps[b][0:M, 0:nc_], lhsT=l, rhs=r_, start=st, stop=sp_), r=rkeys, w=[("ps", bank)])

    def rope_pair(b1, b2, cosT, sinT, M, out1, out2, k1, k2, tabkeys):
        P.op("dve", lambda e: e.tensor_tensor(out=tmp[0][0:M, :], in0=ps[b1][0:M, :], in1=cosT[0:M, :], op=ALU.mult), r=[("ps", b1)] + tabkeys, w=["tmp0"])
        P.op("dve", lambda e: e.tensor_tensor(out=tmp[1][0:M, :], in0=ps[b2][0:M, :], in1=sinT[0:M, :], op=ALU.mult), r=[("ps", b2)] + tabkeys, w=["tmp1"])
        P.op("dve", lambda e: e.tensor_tensor(out=tmp[2][0:M, :], in0=ps[b2][0:M, :], in1=cosT[0:M, :], op=ALU.mult), r=[("ps", b2)] + tabkeys, w=["tmp2"])
        P.op("dve", lambda e: e.tensor_tensor(out=tmp[3][0:M, :], in0=ps[b1][0:M, :], in1=sinT[0:M, :], op=ALU.mult), r=[("ps", b1)] + tabkeys, w=["tmp3"])
        P.op("dve", lambda e: e.tensor_tensor(out=out1, in0=tmp[0][0:M, :], in1=tmp[1][0:M, :], op=ALU.subtract), r=["tmp0", "tmp1"], w=[k1])
        P.op("dve", lambda e: e.tensor_tensor(out=out2, in0=tmp[2][0:M, :], in1=tmp[3][0:M, :], op=ALU.add), r=["tmp2", "tmp3"], w=[k2])

    def sin_reduce(rrbuf, rrkey, src_ang, shift):
        a = src_ang
        if shift != 0.0:
            P.op("dve", lambda e: e.tensor_scalar(out=ang2, in0=src_ang, scalar1=float(shift), scalar2=None, op0=ALU.add), r=["angA"], w=["ang2"])
            a = ang2
        P.op("dve", lambda e, a=a: e.tensor_scalar(out=ki, in0=a, scalar1=float(1.0 / TWO_PI), scalar2=None, op0=ALU.mult), r=["angA", "ang2"], w=["ki"])
        P.op("dve", lambda e: e.tensor_copy(out=kf, in_=ki), r=["ki"], w=["kf"])
        P.op("dve", lambda e, a=a: e.scalar_tensor_tensor(out=rrbuf, in0=kf, scalar=-C1, in1=a, op0=ALU.mult, op1=ALU.add), r=["kf", "angA", "ang2"], w=rrkey)
        P.op("dve", lambda e: e.scalar_tensor_tensor(out=rrbuf, in0=kf, scalar=-C2, in1=rrbuf, op0=ALU.mult, op1=ALU.add), r=["kf"] + rrkey, w=rrkey)
        P.op("dve", lambda e: e.tensor_scalar(out=rrbuf, in0=rrbuf, scalar1=PI_LO, scalar2=-PI_LO, op0=ALU.min, op1=ALU.max), r=rrkey, w=rrkey)

    def store(dst, src, rkeys, wkeys_):
        P.dma("sp", lambda e, a=dst, b=src: e.dma_start(out=a, in_=b), r=rkeys, w=wkeys_)

    TAB = (COSD, SIND, COSM, SINM)
    TABK = [("tab", i) for i in range(4)]
    RR = (WS[0][0][:, 0:512], WS[0][0][:, 512:1024], WS[0][1][:, 0:512], WS[0][1][:, 512:1024])
    RRK = [[("rr", i), ("ws", i // 2)] for i in range(4)]

    def load_x(s, g):
        gi_ = s * NG + g
        xsl_ = gi_ % 2
        xb_ = xbA[xsl_]
        tsl_ = slice(g * 512, g * 512 + 512)
        for k in range(8):
            slot = k % 4
            P.dma("sp", lambda e, k=k, slot=slot: e.dma_start(out=xs[slot], in_=xT[s, k * 128:(k + 1) * 128, tsl_]), w=[("xs", slot)])
            P.op("dve", lambda e, k=k, slot=slot: e.tensor_copy(out=xb_[:, k, :], in_=xs[slot]), r=[("xs", slot)], w=[("xb", xsl_, k)])
        store(xbs[s, :, tsl_].rearrange("(k p) t -> p k t", p=128), xb_, [("xb", xsl_, k) for k in range(8)], [("xbs", s, g)])

    def tables_dve(s, g):
        tsl_ = slice(g * 512, g * 512 + 512)
        P.dma("sp", lambda e: e.dma_start(out=posi, in_=pos[s:s + 1, tsl_].broadcast_to([128, 512])), w=["posi"])
        P.op("dve", lambda e: e.tensor_copy(out=posf, in_=posi), r=["posi"], w=["posf"])
        for ti_, col in ((0, INVFD), (2, INVFM)):
            P.op("dve", lambda e, col=col: e.tensor_scalar(out=angA, in0=posf, scalar1=sm[:, col:col + 1], scalar2=None, op0=ALU.mult),
                 r=["posf", "small"], w=["angA"])
            sin_reduce(RR[ti_], RRK[ti_], angA, math.pi / 2.0)
            sin_reduce(RR[ti_ + 1], RRK[ti_ + 1], angA, 0.0)

    def tables_act():
        for i in range(4):
            P.op("act", lambda e, i=i: e.activation(out=TAB[i], in_=RR[i], func=AF.Sin), r=RRK[i], w=[TABK[i]])

    load_x(0, 0)
    load_cast(WA, wa, 8, NWA, tag="WA", o_list=[1024, 0])
    load_cast(WV, wv, 8, 512, tag="WV")
    load_cast(WUQ, wuq, 3, 768, scale_col=GQ, tag="WUQ")
    load_cast(WUKV, wukv, 2, 1024, scale_col=GKV, tag="WUKV")
    tables_dve(0, 0)

    for s in range(NSEQ):
        for g in range(NG):
            t0 = g * 512
            tsl = slice(t0, t0 + 512)
            gidx = s * NG + g
            kCOSD, kSIND, kCOSM, kSINM = TABK
            xsl = gidx % 2
            xb = xbA[xsl]
            xbk = [("xb", xsl, k) for k in range(8)]
            def fm_chunk(c, M=128):
                b = next_bank()
                mm_group(b, M, [WA[:, k, c * 128:c * 128 + M] for k in range(8)], [xb[:, k, :] for k in range(8)],
                         xbk + [("WA", k, (c * 128 // 1024) * 1024) for k in range(8)])
                return b
            for (c0, nch, fbuf, nbuf, nm, dim) in ((8, 3, cqf, cqn, "cq", 384.0), (11, 2, ckvf, ckvn, "ckv", 256.0)):
                sqs = []
                for j in range(nch):
                    b = fm_chunk(c0 + j)
                    qi = sq_rr[0]; sq_rr[0] = (qi + 1) % 2
                    P.op("act", lambda e, b=b, qi=qi: e.activation(out=sqb[qi], in_=ps[b], func=AF.Square), r=[("ps", b)], w=[("sqb", qi)])
                    P.op("act", lambda e, b=b, j=j, fbuf=fbuf: e.activation(out=fbuf[:, j, :], in_=ps[b], func=AF.Copy), r=[("ps", b)], w=[(nm + "f", j)])
                    sqs.append(qi)
                    if j == 0:
                        bs = next_bank()
                    P.op("pe", lambda e, bs=bs, qi=qi, st=(j == 0), sp_=(j == nch - 1): e.matmul(ps[bs], lhsT=ones, rhs=sqb[qi], start=st, stop=sp_),
                         r=[("sqb", qi), "ones"], w=[("ps", bs)])
                P.op("act", lambda e, bs=bs, dim=dim: e.activation(out=lnt, in_=ps[bs], func=AF.Ln, scale=1.0 / dim, bias=1e-6), r=[("ps", bs)], w=["lnt"])
                P.op("act", lambda e: e.activation(out=rstd, in_=lnt, func=AF.Exp, scale=-0.5), r=["lnt"], w=["rstd"])
                for j in range(nch):
                    P.op("dve", lambda e, j=j, fbuf=fbuf, nbuf=nbuf: e.tensor_tensor(out=nbuf[:, j, :], in0=fbuf[:, j, :], in1=rstd, op=ALU.mult),
                         r=[(nm + "f", j), "rstd"], w=[(nm + "n", j)])
            cqk = [("cqn", j) for j in range(3)]
            ckvk = [("ckvn", j) for j in range(2)]
            tables_act()
            if gidx + 1 < NSEQ * NG:
                load_x((gidx + 1) // NG, (gidx + 1) % NG)
            for c in range(2, 8):
                b = fm_chunk(c)
                i = next_stg()
                P.op("act", lambda e, b=b, i=i: e.activation(out=stg[i], in_=ps[b], func=AF.Copy), r=[("ps", b)], w=[("stg", i)])
                store(fmd[s, c * 128:(c + 1) * 128, tsl], stg[i], [("stg", i)], [("fmd", s, g, c)])
            for tt in range(4):
                b = next_bank()
                mm_group(b, 128, [xb[:, k, tt * 128:(tt + 1) * 128] for k in range(8)], [WV[:, k, :] for k in range(8)], xbk + kWV)
                P.op("act", lambda e, b=b, tt=tt: e.activation(out=vdt[:, tt, :], in_=ps[b], func=AF.Copy), r=[("ps", b)], w=[("vdt", tt)])
            store(vscr[s, 0, :, g * 2048:(g + 1) * 2048], vdt.rearrange("p a b -> p (a b)"), [("vdt", tt) for tt in range(4)], [("vscr", s, 0, g)])

            b1 = fm_chunk(0)
            b2 = fm_chunk(1)
            i1, i2 = next_stg(), next_stg()
            rope_pair(b1, b2, COSD, SIND, 128, stg[i1], stg[i2], ("stg", i1), ("stg", i2), [kCOSD, kSIND])
            store(fmd[s, 0:128, tsl], stg[i1], [("stg", i1)], [("fmd", s, g, 0)])
            store(fmd[s, 128:256, tsl], stg[i2], [("stg", i2)], [("fmd", s, g, 1)])
            b = fm_chunk(13, M=48)
            i = next_stg()
            tk = [kCOSM, kSINM]
            P.op("dve", lambda e, b=b: e.tensor_tensor(out=tmp[0][0:16, :], in0=ps[b][0:16, :], in1=COSM[0:16, :], op=ALU.mult), r=[("ps", b)] + tk, w=["tmp0"])
            P.op("dve", lambda e, b=b: e.tensor_tensor(out=tmp[1][0:16, :], in0=SINM[0:16, :], in1=ps[b][32:48, :], op=ALU.mult), r=[("ps", b)] + tk, w=["tmp1"])
            P.op("dve", lambda e, b=b: e.tensor_tensor(out=tmp[2][32:48, :], in0=ps[b][32:48, :], in1=COSM[32:48, :], op=ALU.mult), r=[("ps", b)] + tk, w=["tmp2"])
            P.op("dve", lambda e, b=b: e.tensor_tensor(out=tmp[3][32:48, :], in0=SINM[32:48, :], in1=ps[b][0:16, :], op=ALU.mult), r=[("ps", b)] + tk, w=["tmp3"])
            P.op("dve", lambda e, i=i: e.tensor_tensor(out=stg[i][0:16, :], in0=tmp[0][0:16, :], in1=tmp[1][0:16, :], op=ALU.subtract), r=["tmp0", "tmp1"], w=[("stg", i)])
            P.op("dve", lambda e, i=i: e.tensor_tensor(out=stg[i][32:48, :], in0=tmp[2][32:48, :], in1=tmp[3][32:48, :], op=ALU.add), r=["tmp2", "tmp3", ("stg", i)], w=[("stg", i)])
            store(fmm[s, 1280:1296, tsl], stg[i][0:16, :], [("stg", i)], [("fmm", s, g, "kr1")])
            store(fmm[s, 1296:1312, tsl], stg[i][32:48, :], [("stg", i)], [("fmm", s, g, "kr2")])
            def q_chunk(qc):
                b = next_bank()
                mm_group(b, 128, [WUQ[:, k, qc * 128:(qc + 1) * 128] for k in range(3)], [cqn[:, k, :] for k in range(3)], cqk + kWUQ)
                return b
            b1 = q_chunk(0)
            b2 = q_chunk(1)
            i1, i2 = next_stg(), next_stg()
            rope_pair(b1, b2, COSM, SINM, 128, stg[i1], stg[i2], ("stg", i1), ("stg", i2), tk)
            store(fmm[s, 0:128, tsl], stg[i1], [("stg", i1)], [("fmm", s, g, 0)])
            store(fmm[s, 128:256, tsl], stg[i2], [("stg", i2)], [("fmm", s, g, 1)])
            for qc in range(2, 6):
                b = q_chunk(qc)
                i = next_stg()
                P.op("act", lambda e, b=b, i=i: e.activation(out=stg[i], in_=ps[b], func=AF.Copy), r=[("ps", b)], w=[("stg", i)])
                store(fmm[s, qc * 128:(qc + 1) * 128, tsl], stg[i], [("stg", i)], [("fmm", s, g, qc)])
            for kc in range(4):
                b = next_bank()
                mm_group(b, 128, [WUKV[:, k, kc * 128:(kc + 1) * 128] for k in range(2)], [ckvn[:, k, :] for k in range(2)], ckvk + kWUKV)
                i = next_stg()
                P.op("act", lambda e, b=b, i=i: e.activation(out=stg[i], in_=ps[b], func=AF.Copy), r=[("ps", b)], w=[("stg", i)])
                store(fmm[s, 768 + kc * 128:768 + (kc + 1) * 128, tsl], stg[i], [("stg", i)], [("fmm", s, g, 6 + kc)])
            for tt in range(4):
                b = next_bank()
                mm_group(b, 128, [ckvn[:, k, tt * 128:(tt + 1) * 128] for k in range(2)], [WUKV[:, k, 512:1024] for k in range(2)], ckvk + kWUKV)
                P.op("dve", lambda e, b=b, tt=tt: e.tensor_copy(out=vmt[:, tt, :], in_=ps[b]), r=[("ps", b)], w=[("vmt", tt)])
            store(vscr[s, 1, :, g * 2048:(g + 1) * 2048], vmt.rearrange("p a b -> p (a b)"), [("vmt", tt) for tt in range(4)], [("vscr", s, 1, g)])
            if gidx + 1 < NSEQ * NG:
                tables_dve((gidx + 1) // NG, (gidx + 1) % NG)
    P.barrier()

    ABF.reset(); AFF.reset()
    WS[0] = [AFF.take(1024) for _ in range(2)]
    VDs = [v3(ABF.take(32 * 128), 32, 128) for _ in range(2)]
    VM = ABF.take(32 * 8 * 128).rearrange("p (t h d) -> p t h d", t=32, h=8, d=128)
    QT = [v3(ABF.take(2 * S), 2, S) for _ in range(2)]
    KT = [ABF.take(S) for _ in range(2)]
    NPT = 4
    PTv = [AFF.take(512).bitcast(BF16).rearrange("p (a b) -> p a b", a=2, b=512) for _ in range(NPT)]
    ostg = [ABF.take(512) for _ in range(2)]
    sqeA = [ABF.take(512) for _ in range(2)]
    EA = [[AFF.take(512) for _ in range(5)] for _ in range(2)]
    Em = [AFF.take(512) for _ in range(2)]
    P.op("pool", lambda e: e.memset(VM.rearrange("p t h d -> p (t h d)"), 1.0), w=["VM"])
    sc_rr = [0]
    pt_rr = [0]
    og_rr = [0]
    acc_rr = [0]

    def load_qk(s, kind, h, slot):
        Q, K = QT[slot], KT[slot]
        if kind == "d":
            if h < 2:
                P.op("pool", lambda e, Q=Q: e.memset(Q[64:128, 0, :], 0.0), w=[("Q", slot)])
                P.op("pool", lambda e, Q=Q: e.memset(Q[0:64, 1, :], 0.0), w=[("Q", slot)])
            P.dma("sp", lambda e, h=h: e.dma_start(out=VDs[slot], in_=vscr[s, 0, :, :].rearrange("p (t c) -> p t c", t=32, c=512)[:, :, h * 128:(h + 1) * 128]),
                  w=[("VD", slot)])
            for m in range(2):
                j = 2 * h + m
                for (dst0, n, qrow, krow) in ((0, 8, j * 8, 64 + j * 8), (8, 8, 128 + j * 8, 192 + j * 8), (16, 48, 256 + j * 48, 640 + j * 48)):
                    p0 = m * 64 + dst0
                    P.dma("sp", lambda e, p0=p0, n=n, qrow=qrow, Q=Q, m=m: e.dma_start(out=Q[p0:p0 + n, m, :], in_=fmd[s, qrow:qrow + n, :]), w=[("Q", slot)])
                    P.dma("sp", lambda e, p0=p0, n=n, krow=krow, K=K: e.dma_start(out=K[p0:p0 + n, :], in_=fmd[s, krow:krow + n, :]), w=[("K", slot)])
        else:
            for (p0, n, qrow, krow) in ((0, 64, 256 + h * 64, 768 + h * 64), (64, 16, h * 16, 1280), (80, 16, 128 + h * 16, 1296)):
                P.dma("sp", lambda e, p0=p0, n=n, qrow=qrow, Q=Q: e.dma_start(out=Q[p0:p0 + n, 0, :], in_=fmm[s, qrow:qrow + n, :]), w=[("Q", slot)])
                P.dma("sp", lambda e, p0=p0, n=n, krow=krow, K=K: e.dma_start(out=K[p0:p0 + n, :], in_=fmm[s, krow:krow + n, :]), w=[("K", slot)])

    pair_rr = [0]
    ep_rr = [0]
    pending_ep = []

    def qk_pair(slot, specs, scale, npairs):
        K = KT[slot]
        pi = pair_rr[0] % npairs
        pair_rr[0] += 1
        b0 = 2 * pi
        c0s = []
        for idx, (qm, kd, kb, qg) in enumerate(specs):
            b = b0 + idx
            Q = QT[slot][:, qm, :]
            j = kb - 4 * qg
            c0 = 128 * j if j >= 0 else 0
            P.op("pe", lambda e, b=b, c0=c0, kd=kd, kb=kb, qg=qg, Q=Q, j=j: e.matmul(
                ps[b][:, c0:512], lhsT=K[0:kd, kb * 128:(kb + 1) * 128], rhs=Q[0:kd, qg * 512 + c0:(qg + 1) * 512], start=True, stop=(j < 0)),
                r=[("Q", slot), ("K", slot)], w=[("ps", b)])
            if j >= 0:
                P.op("pe", lambda e, b=b, c0=c0: e.matmul(ps[b][:, c0:c0 + 128], lhsT=ident, rhs=maskb, start=False, stop=True), r=["cbf"], w=[("ps", b)])
            c0s.append(c0)
        cu = min(c0s)
        pt = pt_rr[0]; pt_rr[0] = (pt + 1) % NPT
        P.op("act", lambda e: e.activation(out=PTv[pt][:, :, cu:512], in_=pspair(b0)[:, :, cu:512], func=AF.Exp, scale=scale),
             r=[("ps", b0), ("ps", b0 + 1)], w=[("PT", pt)])
        return pt, c0s

    def flush_ep():
        while pending_ep:
            pending_ep.pop(0)()

    def attn_diff(s, h, qg, slot):
        A = [4, 5]
        R = [6, 7]
        nkb = 4 * qg + 4
        pend = []

        def pv(kb, pt, c0s):
            last = (kb == nkb - 1)
            for m in range(2):
                c0 = c0s[m]
                P.op("pe", lambda e, m=m, c0=c0: e.matmul(ps[A[m]][:, c0:512], lhsT=VDs[slot][:, kb, :], rhs=PTv[pt][:, m, c0:512], start=(kb == 0), stop=last),
                     r=[("PT", pt), ("VD", slot)], w=[("ps", A[m])])
                P.op("pe", lambda e, m=m, c0=c0: e.matmul(ps[R[m]][:, c0:512], lhsT=ones, rhs=PTv[pt][:, m, c0:512], start=(kb == 0), stop=last),
                     r=[("PT", pt), "ones"], w=[("ps", R[m])])

        for kb in range(nkb):
            pt, c0s = qk_pair(slot, [(0, 128, kb, qg), (1, 128, kb, qg)], 0.125, 2)
            pend.append((kb, pt, c0s))
            if len(pend) > 1:
                pv(*pend.pop(0))
            if kb == 2:
                flush_ep()
        while pend:
            pv(*pend.pop(0))
        flush_ep()
        ei = ep_rr[0]; ep_rr[0] = (ei + 1) % 2
        E = EA[ei]
        sqe = sqeA[ei]
        ek = [("E", ei, i) for i in range(5)]
        P.op("act", lambda e: e.activation(out=E[2], in_=ps[A[0]], func=AF.Copy), r=[("ps", A[0])], w=[ek[2]])
        P.op("dve", lambda e: e.reciprocal(out=E[0], in_=ps[R[0]]), r=[("ps", R[0])], w=[ek[0]])
        P.op("act", lambda e: e.activation(out=E[3], in_=ps[A[1]], func=AF.Copy), r=[("ps", A[1])], w=[ek[3]])
        P.op("dve", lambda e: e.reciprocal(out=E[1], in_=ps[R[1]]), r=[("ps", R[1])], w=[ek[1]])
        P.op("dve", lambda e: e.tensor_tensor(out=E[2], in0=E[2], in1=E[0], op=ALU.mult), r=[ek[0], ek[2]], w=[ek[2]])
        P.op("dve", lambda e: e.tensor_tensor(out=E[3], in0=E[3], in1=E[1], op=ALU.mult), r=[ek[1], ek[3]], w=[ek[3]])
        P.op("dve", lambda e: e.scalar_tensor_tensor(out=E[4], in0=E[3], scalar=lw[:, NEGLAM:NEGLAM + 1], in1=E[2], op0=ALU.mult, op1=ALU.add),
             r=[ek[2], ek[3], "neglam"], w=[ek[4]])

        def part2():
            P.op("act", lambda e: e.activation(out=sqe, in_=E[4], func=AF.Square), r=[ek[4]], w=[("sqe", ei)])
            pi = pair_rr[0] % 2
            pair_rr[0] += 1
            bm = 2 * pi
            P.op("pe", lambda e: e.matmul(ps[bm], lhsT=ones, rhs=sqe, start=True, stop=True), r=[("sqe", ei), "ones"], w=[("ps", bm)])
            P.op("act", lambda e: e.activation(out=E[0], in_=ps[bm], func=AF.Ln, scale=1.0 / 128.0, bias=1e-6), r=[("ps", bm)], w=[ek[0]])
            P.op("act", lambda e: e.activation(out=E[1], in_=E[0], func=AF.Exp, scale=-0.5), r=[ek[0]], w=[ek[1]])
            og = og_rr[0]; og_rr[0] = (og + 1) % 2
            P.op("dve", lambda e: e.scalar_tensor_tensor(out=ostg[og], in0=E[4], scalar=lw[:, SUBGS:SUBGS + 1], in1=E[1], op0=ALU.mult, op1=ALU.mult),
                 r=[ek[4], ek[1], "subgs"], w=[("ostg", og)])
            store(oTs[s, h * 128:(h + 1) * 128, qg * 512:(qg + 1) * 512], ostg[og], [("ostg", og)], [("oTs", s, h, qg)])

        pending_ep.append(part2)

    def attn_mla(s, h, qg, slot):
        flush_ep()
        a = 6 + acc_rr[0]; acc_rr[0] = (acc_rr[0] + 1) % 2
        nkb = 4 * qg + 4
        pend = []
        scale = 96.0 ** -0.5

        def pv(kb0, pt, c0s):
            for idx in range(2):
                kb = kb0 + idx
                c0 = c0s[idx]
                P.op("pe", lambda e, idx=idx, kb=kb, c0=c0: e.matmul(ps[a][:, c0:512], lhsT=VM[:, kb, h, :], rhs=PTv[pt][:, idx, c0:512],
                                                                      start=(kb == 0), stop=(kb == nkb - 1)),
                     r=[("PT", pt), "VM"], w=[("ps", a)])

        for kb0 in range(0, nkb, 2):
            pt, c0s = qk_pair(slot, [(0, 96, kb0, qg), (0, 96, kb0 + 1, qg)], scale, 3)
            pend.append((kb0, pt, c0s))
            if len(pend) > 2:
                pv(*pend.pop(0))
        while pend:
            pv(*pend.pop(0))
        ei = a % 2
        P.op("dve", lambda e: e.reciprocal(out=Em[ei][64:128, :], in_=ps[a][64:128, :]), r=[("ps", a)], w=[("Em", ei)])
        og = og_rr[0]; og_rr[0] = (og + 1) % 2
        P.op("dve", lambda e: e.tensor_tensor(out=ostg[og][0:64, :], in0=ps[a][0:64, :], in1=Em[ei][64:128, :], op=ALU.mult),
             r=[("ps", a), ("Em", ei)], w=[("ostg", og)])
        store(oTs[s, 512 + h * 64:512 + (h + 1) * 64, qg * 512:(qg + 1) * 512], ostg[og][0:64, :], [("ostg", og)], [("oTs", s, 4 + h, qg)])

    jobs = []
    for s in range(NSEQ):
        for h in range(4):
            jobs.append((s, "d", h))
        for h in range(8):
            jobs.append((s, "m", h))
    for ji, (s, kind, h) in enumerate(jobs):
        slot = ji % 2
        if ji == 0:
            load_qk(s, kind, h, slot)
        if kind == "d" and h == 0:
            P.dma("sp", lambda e, s=s: e.dma_start(out=VM[:, :, :, 0:64], in_=vscr[s, 1, :, :].rearrange("p (t h d) -> p t h d", t=32, h=8, d=64)), w=["VM"])
        if ji + 1 < len(jobs):
            load_qk(*jobs[ji + 1], (ji + 1) % 2)
        for qg in range(NG):
            if kind == "d":
                attn_diff(s, h, qg, slot)
            else:
                attn_mla(s, h, qg, slot)
    flush_ep()

    P.barrier()

    ABF.reset(); AFF.reset()
    WS[0] = [AFF.take(1024) for _ in range(2)]
    WG = v3(ABF.take(8 * 2048), 8, 2048)
    WOD = v3(ABF.take(4 * 1024), 4, 1024)
    WOM = v3(ABF.take(4 * 1024), 4, 1024)
    WOUT = v3(ABF.take(8 * 1024), 8, 1024)
    load_cast(WG, wg, 8, 2048, tag="WG")
    load_cast(WOD, wod, 4, 1024, tag="WOD")
    load_cast(WOM, wom, 4, 1024, tag="WOM")
    load_cast(WOUT, wout, 8, 1024, tag="WOUT")
    kWG = wkeys("WG", 8, 2048); kWOD = wkeys("WOD", 4, 1024); kWOM = wkeys("WOM", 4, 1024); kWOUT = wkeys("WOUT", 8, 1024)
    xb2A = [v3(ABF.take(8 * 512), 8, 512) for _ in range(2)]
    oTgA = [v3(ABF.take(8 * 512), 8, 512) for _ in range(2)]
    yTA = [v3(ABF.take(8 * 512), 8, 512) for _ in range(2)]
    x1bA = [ABF.take(1024) for _ in range(2)]
    x1TA = [v3(ABF.take(8 * 128), 8, 128) for _ in range(2)]
    xtokA = [AFF.take(1024) for _ in range(5)]
    xt_rr = [0]
    G0, G1, TT_, UU_ = (AFF.take(512) for _ in range(4))
    rbufA = [AFF.take(1024) for _ in range(2)]
    x1fA = [AFF.take(1024) for _ in range(2)]
    lng = AFF.take(1024)
    lnb = AFF.take(1024)
    st6A = [stt[:, 0, :, :], stt[:, 1, :, :]]
    mvA = [mvt[:, 0, :], mvt[:, 1, :]]
    ln_rr = [0]

    dq = []

    def dq_pop(n=1):
        for _ in range(n):
            if dq:
                dq.pop(0)()

    def dq_flush():
        while dq:
            dq.pop(0)()

    def ln_stats(rin, rkey):
        li = ln_rr[0]; ln_rr[0] = (li + 1) % 2
        st6 = st6A[li]; mv = mvA[li]
        P.op("dve", lambda e: e.bn_stats(out=st6[:, 0, :], in_=rin[:, 0:512]), r=[rkey], w=[("st0", li)])
        P.op("dve", lambda e: e.bn_stats(out=st6[:, 1, :], in_=rin[:, 512:1024]), r=[rkey], w=[("st1", li)])
        P.op("dve", lambda e: e.bn_aggr(out=mv[:, 0:2], in_=st6), r=[("st0", li), ("st1", li)], w=[("mv", li)])
        P.op("act", lambda e: e.activation(out=mv[:, 2:3], in_=mv[:, 1:2], func=AF.Ln, bias=1e-5), r=[("mv", li)], w=[("mv2", li)])
        P.op("act", lambda e: e.activation(out=mv[:, 3:4], in_=mv[:, 2:3], func=AF.Exp, scale=-0.5), r=[("mv2", li)], w=[("mv3", li)])
        return li

    def layer_norm(rin, rkey, gam, bet, gkeys, out, okey, xnb=None, xnbkey=None, tail=None, defer_all=False):
        def stats_and_norm():
            li = ln_stats(rin, rkey)
            mv = mvA[li]
            if xnb is not None:
                P.op("dve", lambda e: e.tensor_scalar(out=xnb, in0=rin, scalar1=mv[:, 0:1], scalar2=mv[:, 3:4], op0=ALU.subtract, op1=ALU.mult),
                     r=[rkey, ("mv", li), ("mv3", li)], w=[xnbkey])
            P.op("dve", lambda e: e.tensor_scalar(out=rin, in0=rin, scalar1=mv[:, 0:1], scalar2=mv[:, 3:4], op0=ALU.subtract, op1=ALU.mult),
                 r=[rkey, ("mv", li), ("mv3", li)], w=[rkey])

        def g_op():
            P.op("dve", lambda e: e.tensor_tensor(out=rin, in0=rin, in1=gam, op=ALU.mult), r=[rkey] + gkeys, w=[rkey])

        def b_op():
            P.op("dve", lambda e: e.tensor_tensor(out=out, in0=rin, in1=bet, op=ALU.add), r=[rkey] + gkeys, w=[okey])
            if tail is not None:
                tail()

        if defer_all:
            dq.append(stats_and_norm)
        else:
            stats_and_norm()
        dq.append(g_op)
        dq.append(b_op)

    P.dma("sp", lambda e: e.dma_start(out=lng, in_=lnp[0:1, :].broadcast_to([128, 1024])), w=["lng"])
    P.dma("sp", lambda e: e.dma_start(out=lnb, in_=lnp[1:2, :].broadcast_to([128, 1024])), w=["lnb"])

    def c_loads(s, g):
        t0 = g * 512
        tsl = slice(t0, t0 + 512)
        gsl = (s * NG + g) % 2
        ctx = dict(s=s, g=g, t0=t0, gsl=gsl, xb2=xb2A[gsl], oTg=oTgA[gsl], yT=yTA[gsl], kxb2=("xb2", gsl), koTg=("oTg", gsl), xts=[])
        P.dma("sp", lambda e, xb2=ctx["xb2"]: e.dma_start(out=xb2, in_=xbs[s, :, tsl].rearrange("(k p) t -> p k t", p=128)), w=[ctx["kxb2"]])
        P.dma("sp", lambda e, oTg=ctx["oTg"]: e.dma_start(out=oTg, in_=oTs[s, :, tsl].rearrange("(k p) t -> p k t", p=128)), w=[ctx["koTg"]])
        return ctx

    def c_fchunk(ctx, f):
        xb2, oTg, yT, kxb2, koTg, gsl = ctx["xb2"], ctx["oTg"], ctx["yT"], ctx["kxb2"], ctx["koTg"], ctx["gsl"]
        ba = next_bank()
        mm_group(ba, 128, [WG[:, k, f * 128:(f + 1) * 128] for k in range(8)], [xb2[:, k, :] for k in range(8)], [kxb2] + kWG)
        bb = next_bank()
        mm_group(bb, 128, [WG[:, k, 1024 + f * 128:1024 + (f + 1) * 128] for k in range(8)], [xb2[:, k, :] for k in range(8)], [kxb2] + kWG)
        bc = next_bank()
        mm_group(bc, 128, [WOD[:, k, f * 128:(f + 1) * 128] for k in range(4)], [oTg[:, k, :] for k in range(4)], [koTg] + kWOD)
        bd = next_bank()
        mm_group(bd, 128, [WOM[:, k, f * 128:(f + 1) * 128] for k in range(4)], [oTg[:, 4 + k, :] for k in range(4)], [koTg] + kWOM)
        P.op("act", lambda e: e.activation(out=G0, in_=ps[ba], func=AF.Sigmoid, bias=sm[:, GATEB + f:GATEB + f + 1]), r=[("ps", ba), "small"], w=["G0"])
        P.op("act", lambda e: e.activation(out=G1, in_=ps[bb], func=AF.Sigmoid, bias=sm[:, GATEB + 8 + f:GATEB + 9 + f]), r=[("ps", bb), "small"], w=["G1"])
        P.op("dve", lambda e: e.tensor_tensor(out=TT_, in0=ps[bc], in1=G0, op=ALU.mult), r=[("ps", bc), "G0"], w=["TT"])
        P.op("dve", lambda e: e.tensor_tensor(out=UU_, in0=ps[bd], in1=G1, op=ALU.mult), r=[("ps", bd), "G1"], w=["UU"])
        P.op("dve", lambda e: e.tensor_tensor(out=yT[:, f, :], in0=TT_, in1=UU_, op=ALU.add), r=["TT", "UU"], w=[("yT", gsl, f)])

    tile_rr = [0]

    def c_tile_a(ctx, tt):
        s, t0, yT, gsl = ctx["s"], ctx["t0"], ctx["yT"], ctx["gsl"]
        ti = tile_rr[0]; tile_rr[0] = (ti + 1) % 2
        xi = xt_rr[0]; xt_rr[0] = (xi + 1) % 5
        P.dma("sp", lambda e: e.dma_start(out=xtokA[xi], in_=x[s, t0 + tt * 128:t0 + (tt + 1) * 128, :]), w=[("xtok", xi)])
        rbuf, x1f, x1b = rbufA[ti], x1fA[ti], x1bA[ti]
        yk = [("yT", gsl, f) for f in range(8)]
        for n in range(2):
            b = next_bank()
            mm_group(b, 128, [yT[:, k, tt * 128:(tt + 1) * 128] for k in range(8)], [WOUT[:, k, n * 512:(n + 1) * 512] for k in range(8)], yk + kWOUT)
            P.op("dve", lambda e, b=b, n=n: e.scalar_tensor_tensor(out=rbuf[:, n * 512:(n + 1) * 512], in0=xtokA[xi][:, n * 512:(n + 1) * 512],
                                                                  scalar=ALPHA, in1=ps[b], op0=ALU.mult, op1=ALU.add),
                 r=[("ps", b), ("xtok", xi)], w=[("rbuf", ti)])
        row0 = s * S + t0 + tt * 128
        layer_norm(rbuf, ("rbuf", ti), lng, lnb, ["lng", "lnb"], x1f, ("x1f", ti), xnb=x1b, xnbkey=("x1b", ti),
                   tail=lambda: store(x1s[row0:row0 + 128, :], x1f, [("x1f", ti)], [("x1s", row0)]))
        return ti

    def c_tile_b(ctx, tt, ti):
        s, t0 = ctx["s"], ctx["t0"]
        x1b, x1T = x1bA[ti], x1TA[ti]
        b = next_bank()
        psb = ps[b].bitcast(BF16)
        for k in range(8):
            P.op("pe", lambda e, k=k: e.transpose(psb[:, k * 128:(k + 1) * 128], x1b[:, k * 128:(k + 1) * 128], ident),
                 r=[("x1b", ti), "cbf"], w=[("ps", b)])
        for k in range(8):
            P.op("act", lambda e, k=k: e.activation(out=x1T[:, k, :], in_=psb[:, k * 128:(k + 1) * 128], func=AF.Identity,
                                                    scale=sm[:, LN1G + k:LN1G + k + 1], bias=sm[:, LN1B + k:LN1B + k + 1]),
                 r=[("ps", b), "small"], w=[("x1T", ti)])
        c0 = s * S + t0 + tt * 128
        P.dma("sp", lambda e: e.dma_start(out=x1Ts[:, c0:c0 + 128].rearrange("(k p) t -> p k t", p=128), in_=x1T), r=[("x1T", ti)], w=[("x1Ts", c0)])

    cgroups = [(s, g) for s in range(NSEQ) for g in range(NG)]
    prev = None
    for gi in range(len(cgroups) + 1):
        cur = c_loads(*cgroups[gi]) if gi < len(cgroups) else None
        for i in range(4):
            ti = None
            if prev is not None:
                ti = c_tile_a(prev, i)
            if cur is not None:
                c_fchunk(cur, 2 * i)
                dq_pop()
                c_fchunk(cur, 2 * i + 1)
                dq_pop()
            else:
                dq_flush()
            if prev is not None:
                c_tile_b(prev, i, ti)
        prev = cur
    dq_flush()

    P.barrier()

    ABF.reset(); AFF.reset()
    WS[0] = [AFF.take(1024) for _ in range(2)]
    WUP = v3(ABF.take(8 * 4096), 8, 4096)
    WDN = v3(ABF.take(32 * 1024), 32, 1024)
    kWUP = wkeys("WUP", 8, 4096); kWDN = wkeys("WDN", 32, 1024)
    x1TgA = [v3(ABF.take(8 * 256), 8, 256) for _ in range(2)]
    hT = [AFF.take(128).bitcast(BF16) for _ in range(3)]
    x1gA = [v3(AFF.take(2 * 1024), 2, 1024) for _ in range(2)]
    sqv = [AFF.take(256) for _ in range(2)]
    rb2A = [AFF.take(1024) for _ in range(2)]
    ob = [AFF.take(1024) for _ in range(2)]
    lng2 = AFF.take(1024)
    lnb2 = AFF.take(1024)
    P.dma("sp", lambda e: e.dma_start(out=lng2, in_=lnp[2:3, :].broadcast_to([128, 1024])), w=["lng2"])
    P.dma("sp", lambda e: e.dma_start(out=lnb2, in_=lnp[3:4, :].broadcast_to([128, 1024])), w=["lnb2"])
    ub_rr = [0]
    NGD = NSEQ * S // 256

    def d_loads(gi):
        r0_ = gi * 256
        P.dma("sp", lambda e: e.dma_start(out=x1TgA[gi % 2], in_=x1Ts[:, r0_:r0_ + 256].rearrange("(k p) t -> p k t", p=128)), w=[("x1Tg", gi % 2)])
        P.dma("sp", lambda e: e.dma_start(out=x1gA[gi % 2], in_=x1s[r0_:r0_ + 256, :].rearrange("(a p) d -> p a d", p=128)), w=[("x1g", gi % 2)])

    d_loads(0)
    for j4 in range(4):
        load_cast(WUP, wup, 8, 4096, tag="WUP", o_list=[j4 * 1024])
        load_cast(WDN, wdn, 32, 1024, tag="WDN", c_list=range(8 * j4, 8 * j4 + 8))
    for gi in range(NGD):
        r0 = gi * 256
        x1Tg = x1TgA[gi % 2]
        x1g = x1gA[gi % 2]
        kx1T = ("x1Tg", gi % 2)
        kx1g = ("x1g", gi % 2)
        pend = []

        def down(fc, hi):
            for tt in range(2):
                for n in range(2):
                    a = 4 + tt * 2 + n
                    P.op("pe", lambda e, a=a, tt=tt, n=n, fc=fc, hi=hi: e.matmul(ps[a], lhsT=hT[hi][:, tt * 128:(tt + 1) * 128], rhs=WDN[:, fc, n * 512:(n + 1) * 512],
                                                                               start=(fc == 0), stop=(fc == 31)),
                         r=[("hT", hi), ("WDN", fc, 0)], w=[("ps", a)])

        for fc in range(32):
            ub = ub_rr[0]; ub_rr[0] = (ub + 1) % 4
            mm_group(ub, 128, [WUP[:, k, fc * 128:(fc + 1) * 128] for k in range(8)], [x1Tg[:, k, :] for k in range(8)],
                     [kx1T] + [("WUP", k, (fc * 128 // 1024) * 1024) for k in range(8)], ncols=256)
            qi = fc % 2
            hi = fc % 3
            P.op("act", lambda e, ub=ub, qi=qi: e.activation(out=sqv[qi], in_=ps[ub][:, 0:256], func=AF.Square), r=[("ps", ub)], w=[("sqv", qi)])
            P.op("dve", lambda e, ub=ub, qi=qi, hi=hi: e.scalar_tensor_tensor(out=hT[hi], in0=ps[ub][:, 0:256], scalar=0.0, in1=sqv[qi], op0=ALU.is_gt, op1=ALU.mult),
                 r=[("ps", ub), ("sqv", qi)], w=[("hT", hi)])
            pend.append((fc, hi))
            if len(pend) > 1:
                down(*pend.pop(0))
            if fc % 3 == 2:
                dq_pop()
            if fc == 8 and gi + 1 < NGD:
                d_loads(gi + 1)
        while pend:
            down(*pend.pop(0))
        dq_flush()
        for tt in range(2):
            for n in range(2):
                a = 4 + tt * 2 + n
                P.op("dve", lambda e, a=a, tt=tt, n=n, x1g=x1g: e.scalar_tensor_tensor(out=rb2A[tt][:, n * 512:(n + 1) * 512], in0=x1g[:, tt, n * 512:(n + 1) * 512],
                                                                             scalar=ALPHA, in1=ps[a], op0=ALU.mult, op1=ALU.add),
                     r=[("ps", a), kx1g], w=[("rb2", tt)])
        for tt in range(2):
            layer_norm(rb2A[tt], ("rb2", tt), lng2, lnb2, ["lng2", "lnb2"], ob[tt], ("ob", tt), defer_all=True,
                       tail=lambda tt=tt, r0=r0: store(y[r0 + tt * 128:r0 + (tt + 1) * 128, :], ob[tt], [("ob", tt)], [("y", r0, tt)]))
    dq_flush()

    P.barrier()

    with nc.Block() as block:
        semh = {}
        import contextlib
        with contextlib.ExitStack() as es:
            for e_ in ("pe", "act", "dve", "pool"):
                semh[e_] = es.enter_context(nc.semaphore("c_" + e_))
            for i in range(NDMA):
                semh[("dma", i)] = es.enter_context(nc.semaphore(f"d{i}"))

            @block.sync
            def _(e):
                P.emit("sp", e, semh)

            @block.tensor
            def _(e):
                P.emit("pe", e, semh)

            @block.scalar
            def _(e):
                P.emit("act", e, semh)

            @block.vector
            def _(e):
                P.emit("dve", e, semh)

            @block.gpsimd
            def _(e):
                P.emit("pool", e, semh)
    nc._prog_marks = P.marks
    return nc


def _chunked(w, C):
    K, N = w.shape
    assert K == C * 128
    return np.ascontiguousarray(w.reshape(C, 128, N).transpose(1, 0, 2).reshape(128, C * N))


_NC_CACHE = {}


def kernel(x, positions, w_in, gate_b, diff_lambda, diff_subln_g, mla_q_norm_g, w_uq,
           mla_kv_norm_g, w_ukv, w_o_diff, w_o_mla, w_out, ln1_g, ln1_b, w_up, w_down,
           ln2_g, ln2_b):
    f32 = np.float32
    x = np.asarray(x, f32)
    positions = np.asarray(positions, np.int32)
    w_in0 = np.asarray(w_in, f32)[0]
    j = np.arange(8)
    i8 = np.arange(8)
    dq_x1 = (64 * j[:, None] + i8[None, :]).reshape(-1)
    dq_x2 = dq_x1 + 8
    dq_np = (64 * j[:, None] + 16 + np.arange(48)[None, :]).reshape(-1)
    cols = np.concatenate([
        dq_x1, 512 + dq_x1,
        dq_x2, 512 + dq_x2,
        dq_np, 512 + dq_np,
        np.arange(1536, 1920),
        np.arange(1920, 2176),
        np.arange(2176, 2192), np.arange(2176, 2192), np.arange(2192, 2208),
    ])
    assert cols.size == NWA
    wa = _chunked(w_in0[:, cols], 8)
    wv = _chunked(w_in0[:, 1024:1536], 8)
    wg = _chunked(w_in0[:, 2208:4256], 8)
    h8 = np.arange(8)
    i16 = np.arange(16)
    uq_cols = np.concatenate([
        (96 * h8[:, None] + 64 + i16[None, :]).reshape(-1),
        (96 * h8[:, None] + 80 + i16[None, :]).reshape(-1),
        (96 * h8[:, None] + np.arange(64)[None, :]).reshape(-1),
    ])
    wuq = _chunked(np.asarray(w_uq, f32)[0][:, uq_cols], 3)
    ukv_cols = np.concatenate([
        (128 * h8[:, None] + np.arange(64)[None, :]).reshape(-1),
        (128 * h8[:, None] + 64 + np.arange(64)[None, :]).reshape(-1),
    ])
    wukv = _chunked(np.asarray(w_ukv, f32)[0][:, ukv_cols], 2)
    wod = _chunked(np.asarray(w_o_diff, f32)[0], 4)
    wom = _chunked(np.asarray(w_o_mla, f32)[0], 4)
    woutc = _chunked(np.asarray(w_out, f32)[0], 8)
    wupc = _chunked(np.asarray(w_up, f32)[0], 8)
    wdnc = _chunked(np.asarray(w_down, f32)[0], 32)
    small = np.zeros((128, 48), f32)
    small[:, 0:3] = np.asarray(mla_q_norm_g, f32)[0].reshape(3, 128).T
    small[:, 3:5] = np.asarray(mla_kv_norm_g, f32)[0].reshape(2, 128).T
    small[:, 5] = np.asarray(diff_subln_g, f32)[0]
    p = np.arange(128)
    small[:, 6] = np.power(np.float32(500000.0), -(p % 8).astype(f32) / np.float32(8)).astype(f32)
    small[:, 7] = np.power(np.float32(500000.0), -(p % 16).astype(f32) / np.float32(16)).astype(f32)
    gb = np.asarray(gate_b, f32)[0]
    small[:, 8:24] = gb.reshape(2, 8, 128).transpose(2, 0, 1).reshape(128, 16)
    small[:, 24:32] = np.asarray(ln1_g, f32)[0].reshape(8, 128).T
    small[:, 32:40] = np.asarray(ln1_b, f32)[0].reshape(8, 128).T
    lam = np.asarray(diff_lambda, f32)[0].reshape(1, 256)
    lnp = np.stack([np.asarray(a, f32)[0] for a in (ln1_g, ln1_b, ln2_g, ln2_b)], 0)
    cb = np.zeros((128, 256), f32)
    kk = np.arange(128)
    cb[:, 0:128] = np.where(kk[None, :] >= kk[:, None], 0.0, -30000.0)
    cb[:, 128:256] = np.eye(128)
    cbf = cb.astype(ml_dtypes.bfloat16)

    if "nc" not in _NC_CACHE:
        _NC_CACHE["nc"] = build()
    nc = _NC_CACHE["nc"]

    shared = dict(wa=wa, wv=wv, wg=wg, wuq=wuq, wukv=wukv, wod=wod, wom=wom, wout=woutc, wup=wupc, wdn=wdnc,
                  small=small, lam=lam, lnp=np.ascontiguousarray(lnp), cbf=cbf)
    in_maps = []
    for c in range(8):
        xc = np.ascontiguousarray(x[2 * c:2 * c + 2])
        m = dict(shared)
        m["x"] = xc
        m["xT"] = np.ascontiguousarray(xc.transpose(0, 2, 1))
        m["pos"] = np.ascontiguousarray(positions[2 * c:2 * c + 2])
        in_maps.append(m)
    res = run_bass_kernel_spmd(nc, in_maps, core_ids=list(range(8)))
    outs = [np.asarray(r["y"], f32).reshape(NSEQ, S, D) for r in res.results]
    return np.concatenate(outs, axis=0)
```
